# Optimizing a Trainium2 kernel written in Bass

```python
import math
import jax, jax.numpy as jnp
from jax import lax
import numpy as np

D_MODEL = 2048
BATCH = 4
SEQ = 4096
DEPTH = 2

CTX_LEN = 256
GRID_W = 64
D_MIX = D_MODEL
MLA_HEADS = 8
MLA_NOPE = 128
MLA_ROPE = 64
MLA_V = 128
MLA_W = MLA_HEADS * MLA_V
Q_LORA = 512
KV_LORA = 256
ROPE_THETA = 10000.0
Q_BLOCK = 128
SCALE = (MLA_NOPE + MLA_ROPE) ** -0.5
HY_W = D_MIX - MLA_W
CONV_W = 3
FILT_EMB = 33
FILT_HIDDEN = 64
FILT_TARGET = 1e-2
FILT_FAST_DECAY = 0.3
FILT_SLOW_DECAY = 1.5
EPS = 1e-6
OFF_Q = 0
OFF_KV = OFF_Q + Q_LORA
OFF_KR = OFF_KV + KV_LORA
OFF_GM = OFF_KR + MLA_ROPE
OFF_HY = OFF_GM + MLA_W
OFF_GH = OFF_HY + 3 * HY_W
N_IN = OFF_GH + HY_W

kernel_name = "hymba_mla_hyena_prefix_dit"

F32 = jnp.float32


def rmsnorm(x, g):
    xf = x.astype(F32)
    y = xf * lax.rsqrt(jnp.mean(xf * xf, axis=-1, keepdims=True) + EPS)
    return (y * g.astype(F32)).astype(x.dtype)


def _modulation(cvec, ada_w, ada_b):
    m = jax.nn.silu(cvec) @ ada_w + ada_b
    return jnp.split(m, 3, axis=-1)


def _axial_rope_tables(n):
    rows = n // GRID_W
    row = jnp.repeat(jnp.arange(rows, dtype=F32), GRID_W)
    col = jnp.tile(jnp.arange(GRID_W, dtype=F32), rows)
    nf = MLA_ROPE // 4
    inv = ROPE_THETA ** (-jnp.arange(nf, dtype=F32) / nf)
    ang = jnp.stack([row[:, None] * inv, col[:, None] * inv], axis=1)
    return jnp.cos(ang), jnp.sin(ang)


def apply_axial_rope(x, cos, sin):
    xr = x.astype(F32).reshape(x.shape[:-1] + (2, 2, MLA_ROPE // 4))
    x1, x2 = xr[..., 0, :], xr[..., 1, :]
    out = jnp.stack([x1 * cos - x2 * sin, x2 * cos + x1 * sin], axis=-2)
    return out.reshape(x.shape).astype(x.dtype)


def _mla_q(p_q, lp):
    q = rmsnorm(p_q, lp["q_norm_g"]) @ lp["w_uq"]
    q = q.reshape(p_q.shape[:-1] + (MLA_HEADS, MLA_NOPE + MLA_ROPE))
    return q[..., :MLA_NOPE], q[..., MLA_NOPE:]


def _mla_kv(p_kv, lp):
    c_kv = rmsnorm(p_kv[..., :KV_LORA], lp["kv_norm_g"])
    kv = (c_kv @ lp["w_ukv"]).reshape(p_kv.shape[:-1] + (MLA_HEADS, MLA_NOPE + MLA_V))
    return kv[..., :MLA_NOPE], kv[..., MLA_NOPE:], p_kv[..., KV_LORA:]


def _attend(q_nope, q_rope, k_nope, k_rope, v):
    s = jnp.einsum("bqhd,bkhd->bhqk", q_nope, k_nope, preferred_element_type=F32)
    s = s + jnp.einsum("bqhr,bkr->bhqk", q_rope, k_rope, preferred_element_type=F32)
    p = jax.nn.softmax(s * SCALE, axis=-1)
    return jnp.einsum("bhqk,bkhd->bqhd", p.astype(v.dtype), v)


def _short_conv(u, w, b):
    L = u.shape[1]
    pad = CONV_W // 2
    up = jnp.pad(u, ((0, 0), (pad, pad), (0, 0)))
    out = b
    for j in range(CONV_W):
        out = out + up[:, j:j + L] * w[j]
    return out


def _implicit_filters(L, lp):
    t = jnp.linspace(0.0, 1.0, L, dtype=F32)[:, None]
    bands = (FILT_EMB - 1) // 2
    f = jnp.linspace(1e-4, bands - 1, bands, dtype=F32)[None, :]
    wpos = (2.0 * math.pi) * jnp.arange(L, dtype=F32)[:, None] / L
    z = jnp.concatenate([t, jnp.cos(f * wpos), -jnp.sin(f * wpos)], axis=-1)
    h = jnp.sin(lp["filt_freq"] * (z @ lp["filt_w1"] + lp["filt_b1"]))
    h = jnp.sin(lp["filt_freq"] * (h @ lp["filt_w2"] + lp["filt_b2"]))
    h = (h @ lp["filt_w3"]).astype(F32).reshape(L, 2, HY_W)
    deltas = jnp.linspace(math.log(FILT_TARGET) / FILT_FAST_DECAY,
                          math.log(FILT_TARGET) / FILT_SLOW_DECAY, HY_W, dtype=F32)
    decay = jnp.exp(-t * jnp.abs(deltas))
    h = h * decay[:, None, :]
    return h[:, 0], h[:, 1]


def _bidir_long_conv(u, h_f, h_b, d_bias):
    L = u.shape[1]
    k = jnp.concatenate([h_f, jnp.zeros((1, HY_W), F32), h_b[1:][::-1]], axis=0)
    kf = jnp.fft.rfft(k, n=2 * L, axis=0)
    uf32 = u.astype(F32)
    uf = jnp.fft.rfft(uf32, n=2 * L, axis=1)
    y = jnp.fft.irfft(uf * kf[None], n=2 * L, axis=1)[:, :L]
    return (y + uf32 * d_bias.astype(F32)).astype(u.dtype)


def _hyena(p_hy, lp):
    L = p_hy.shape[1]
    u = _short_conv(p_hy, lp["conv_w"], lp["conv_b"])
    x0, x1, v = jnp.split(u, 3, axis=-1)
    h_f, h_b = _implicit_filters(L, lp)
    return x0 * _bidir_long_conv(x1 * v, h_f, h_b, lp["hy_D"])


def _branch_merge(o_mla, p, lp):
    g_m = jax.nn.silu(p[..., OFF_GM:OFF_HY])
    g_h = jax.nn.silu(p[..., OFF_GH:N_IN])
    y_h = _hyena(p[..., OFF_HY:OFF_GH], lp)
    y = jnp.concatenate([rmsnorm(o_mla, lp["grp_g_mla"]) * g_m,
                         rmsnorm(y_h, lp["grp_g_hy"]) * g_h], axis=-1)
    return rmsnorm(y @ lp["w_out"], lp["post_g"])


def _layer(x, ctx, c, c_ctx, lp, cos, sin, update_ctx):
    B, L, _ = x.shape
    Lc = ctx.shape[1]
    sh_x, sc_x, g_x = [m[:, None, :] for m in _modulation(c, lp["ada_w"], lp["ada_b"])]
    sh_c, sc_c, g_c = _modulation(c_ctx, lp["ada_w"], lp["ada_b"])
    hx = rmsnorm(x, lp["pre_g"]) * (1.0 + sc_x) + sh_x
    hc = rmsnorm(ctx, lp["pre_g"]) * (1.0 + sc_c) + sh_c
    px = hx @ lp["w_in"]
    if update_ctx:
        pc = hc @ lp["w_in"]
        pc_kv = pc[..., OFF_KV:OFF_GM]
    else:
        pc_kv = hc @ lp["w_in"][:, OFF_KV:OFF_GM]
    kn_c, v_c, kr_c = _mla_kv(pc_kv, lp)
    kn_x, v_x, kr_x = _mla_kv(px[..., OFF_KV:OFF_GM], lp)
    kr_x = apply_axial_rope(kr_x, cos, sin)
    k_nope = jnp.concatenate([kn_c, kn_x], axis=1)
    k_rope = jnp.concatenate([kr_c, kr_x], axis=1)
    v = jnp.concatenate([v_c, v_x], axis=1)
    qn_x, qr_x = _mla_q(px[..., OFF_Q:OFF_KV], lp)
    qr_x = apply_axial_rope(qr_x, cos[:, None], sin[:, None])
    nb = L // Q_BLOCK

    def to_blocks(q):
        return q.reshape((B, nb, Q_BLOCK) + q.shape[2:]).swapaxes(0, 1)

    o_x = lax.map(lambda qs: _attend(qs[0], qs[1], k_nope, k_rope, v),
                  (to_blocks(qn_x), to_blocks(qr_x)))
    o_x = o_x.swapaxes(0, 1).reshape(B, L, MLA_W)
    x_new = x + g_x * _branch_merge(o_x, px, lp)
    if update_ctx:
        qn_c, qr_c = _mla_q(pc[..., OFF_Q:OFF_KV], lp)
        o_c = _attend(qn_c, qr_c, kn_c, kr_c, v_c).reshape(B, Lc, MLA_W)
        ctx = ctx + g_c * _branch_merge(o_c, pc, lp)
    return x_new, ctx


def setup_inputs(seed: int = 0) -> dict:
    key = jax.random.key(seed)
    ks = iter(jax.random.split(key, 32))

    def nrm(shape, s):
        return jax.random.normal(next(ks), shape, F32) * s

    def gain(shape):
        return 1.0 + nrm(shape, 0.05)

    return {
        "x": nrm((BATCH, SEQ, D_MODEL), 1.0),
        "c": nrm((BATCH, D_MODEL), 1.0),
        "ctx": nrm((BATCH, CTX_LEN, D_MODEL), 1.0),
        "c_ctx": nrm((D_MODEL,), 1.0),
        "ada_w": nrm((DEPTH, D_MODEL, 3 * D_MODEL), 0.5 * D_MODEL ** -0.5),
        "ada_b": nrm((DEPTH, 3 * D_MODEL), 0.01),
        "pre_g": gain((DEPTH, D_MODEL)),
        "w_in": nrm((DEPTH, D_MODEL, N_IN), D_MODEL ** -0.5),
        "q_norm_g": gain((DEPTH, Q_LORA)),
        "w_uq": nrm((DEPTH, Q_LORA, MLA_HEADS * (MLA_NOPE + MLA_ROPE)), Q_LORA ** -0.5),
        "kv_norm_g": gain((DEPTH, KV_LORA)),
        "w_ukv": nrm((DEPTH, KV_LORA, MLA_HEADS * (MLA_NOPE + MLA_V)), KV_LORA ** -0.5),
        "conv_w": nrm((DEPTH, CONV_W, 3 * HY_W), CONV_W ** -0.5),
        "conv_b": nrm((DEPTH, 3 * HY_W), 0.02),
        "filt_w1": nrm((DEPTH, FILT_EMB, FILT_HIDDEN), FILT_EMB ** -0.5),
        "filt_b1": nrm((DEPTH, FILT_HIDDEN), 0.1),
        "filt_freq": gain((DEPTH, FILT_HIDDEN)),
        "filt_w2": nrm((DEPTH, FILT_HIDDEN, FILT_HIDDEN), FILT_HIDDEN ** -0.5),
        "filt_b2": nrm((DEPTH, FILT_HIDDEN), 0.1),
        "filt_w3": nrm((DEPTH, FILT_HIDDEN, 2 * HY_W), FILT_HIDDEN ** -0.5),
        "hy_D": nrm((DEPTH, HY_W), 1.0),
        "grp_g_mla": gain((DEPTH, MLA_W)),
        "grp_g_hy": gain((DEPTH, HY_W)),
        "w_out": nrm((DEPTH, D_MIX, D_MODEL), D_MIX ** -0.5),
        "post_g": gain((DEPTH, D_MODEL)),
    }


def reference(x, c, ctx, c_ctx, ada_w, ada_b, pre_g, w_in, q_norm_g, w_uq, kv_norm_g, w_ukv,
              conv_w, conv_b, filt_w1, filt_b1, filt_freq, filt_w2, filt_b2, filt_w3, hy_D,
              grp_g_mla, grp_g_hy, w_out, post_g):
    cos, sin = _axial_rope_tables(x.shape[1])
    for l in range(DEPTH):
        lp = {
            "ada_w": ada_w[l], "ada_b": ada_b[l], "pre_g": pre_g[l], "w_in": w_in[l],
            "q_norm_g": q_norm_g[l], "w_uq": w_uq[l], "kv_norm_g": kv_norm_g[l], "w_ukv": w_ukv[l],
            "conv_w": conv_w[l], "conv_b": conv_b[l], "filt_w1": filt_w1[l], "filt_b1": filt_b1[l],
            "filt_freq": filt_freq[l], "filt_w2": filt_w2[l], "filt_b2": filt_b2[l],
            "filt_w3": filt_w3[l], "hy_D": hy_D[l], "grp_g_mla": grp_g_mla[l],
            "grp_g_hy": grp_g_hy[l], "w_out": w_out[l], "post_g": post_g[l],
        }
        x, ctx = _layer(x, ctx, c, c_ctx, lp, cos, sin, update_ctx=(l < DEPTH - 1))
    return x
```

```python
import math
import numpy as np
import ml_dtypes
import concourse.bass as bass
import concourse.mybir as mybir
from concourse.bass_utils import run_bass_kernel_spmd

F32 = mybir.dt.float32
BF16 = mybir.dt.bfloat16
AF = mybir.ActivationFunctionType
ALU = mybir.AluOpType

D = 2048
NIN = 5952
LX = 4096
LC = 256
T = LC + LX
DEPTH = 2
H = 8
EPS = 1e-6
SCALE = 192.0 ** -0.5
OFF_Q, OFF_KV, OFF_KR, OFF_GM, OFF_HY, OFF_GH = 0, 512, 768, 832, 1856, 4928
MAGIC = 12582912.0
TWO_PI = 2.0 * math.pi
NCORES = 4
PIPE_DEPTH = 4
HB = 2
DBG = {}


class Res:
    __slots__ = ("w", "r")

    def __init__(self):
        self.w = None
        self.r = {}


class Eng:
    def __init__(self, name, e, si):
        self.name, self.e, self.si = name, e, si
        self.cnt = 0
        self.seen = {}


class Sched:
    NDS = 48

    def __init__(self, nc):
        self.nc = nc
        self.sems = []

        def mk(name, e):
            self.sems.append(nc.alloc_semaphore("sem_" + name))
            return Eng(name, e, len(self.sems) - 1)

        self.pe = mk("pe", nc.tensor)
        self.act = mk("act", nc.scalar)
        self.dve = mk("dve", nc.vector)
        self.pool = mk("pool", nc.gpsimd)
        self.sp = mk("sp", nc.sync)
        self.engs = [self.pe, self.act, self.dve, self.pool, self.sp]
        self.dq = []
        self.dq_sw = []
        for i in range(self.NDS):
            self.sems.append(nc.alloc_semaphore("sem_d%d" % i))
            (self.dq if i < 24 else self.dq_sw).append([len(self.sems) - 1, 0])
        self.dnext = {0: 0, 1: 0}
        self.rr = 0

    def _wait(self, X, toks):
        for si, val in toks:
            if X is self.pe and si == X.si:
                continue
            if X.seen.get(si, 0) < val:
                X.e.wait_ge(self.sems[si], val)
                X.seen[si] = val

    @staticmethod
    def _deps(reads, writes):
        toks = []
        for r in reads:
            if r.w is not None:
                toks.append(r.w)
        for w in writes:
            if w.w is not None:
                toks.append(w.w)
            toks.extend(w.r.items())
        return toks

    @staticmethod
    def _reg(tok, reads, writes):
        for r in reads:
            if r.r.get(tok[0], 0) < tok[1]:
                r.r[tok[0]] = tok[1]
        for w in writes:
            w.w = tok
            w.r = {}

    def op(self, X, fn, reads=(), writes=(), inc=True):
        self._wait(X, self._deps(reads, writes))
        ins = fn()
        if inc:
            X.cnt += 1
            ins.then_inc(self.sems[X.si], 1)
            tok = (X.si, X.cnt)
        else:
            tok = (X.si, X.cnt + 1)
        self._reg(tok, reads, writes)
        return ins

    def dma(self, Q, out, in_, reads=(), writes=(), **kw):
        sw = 1 if Q is self.pool else 0
        lst = self.dq_sw if sw else self.dq
        k = self.dnext[sw]
        self.dnext[sw] = (k + 1) % len(lst)
        d = lst[k]
        toks = self._deps(reads, writes)
        if d[1] > 0:
            toks.append((d[0], d[1]))
        self._wait(Q, toks)
        Q.e.dma_start(out=out, in_=in_, **kw).then_inc(self.sems[d[0]], 16)
        d[1] += 16
        self._reg((d[0], d[1]), reads, writes)

    def barrier(self):
        toks = [(E.si, E.cnt) for E in self.engs if E.cnt > 0]
        toks += [(d[0], d[1]) for d in self.dq + self.dq_sw if d[1] > 0]
        for X in self.engs:
            self._wait(X, toks)


def _freq_map(L):
    nt = L // 128
    ne, no = nt // 2 + 1, nt // 2
    fm = -np.ones((ne + no) * 128, np.int64)
    ev = np.arange(0, L + 1, 2)
    od = np.arange(1, L, 2)
    fm[:len(ev)] = ev
    fm[ne * 128:ne * 128 + len(od)] = od
    return fm, ne


def _dft_tables(L):
    N = 2 * L
    fm, _ = _freq_map(L)
    f = np.where(fm < 0, 0, fm)
    s_ = np.arange(L, dtype=np.int64)
    m = (s_[:, None] * f[None, :]) % N
    ang = m.astype(np.float64) * (2.0 * np.pi / N)
    C = np.cos(ang).astype(ml_dtypes.bfloat16)
    Sn = np.sin(ang).astype(ml_dtypes.bfloat16)
    nF = len(fm) // 128
    CS = np.stack([C, Sn], axis=1)
    tabF = np.ascontiguousarray(CS.reshape(L, 2, nF, 128).transpose(0, 2, 1, 3))
    tabT = np.ascontiguousarray(CS.transpose(2, 1, 0))
    return tabF, tabT


def _filter_consts(L):
    t = np.linspace(0.0, 1.0, L, dtype=np.float32)[:, None]
    bands = 16
    f = np.linspace(1e-4, bands - 1, bands, dtype=np.float32)[None, :]
    wpos = (np.float32(2.0 * math.pi) * np.arange(L, dtype=np.float32)[:, None] / np.float32(L)).astype(np.float32)
    z = np.concatenate([t, np.cos(f * wpos), -np.sin(f * wpos)], axis=-1).astype(np.float32)
    idx_rev = (L - np.arange(L)) % L
    zrev = z[idx_rev]
    tt = t[:, 0]
    trev = tt[idx_rev].copy()
    trev[0] = 1e4
    nt = L // 128
    negt = np.stack([(-tt).reshape(nt, 128).T, (-trev).reshape(nt, 128).T], axis=1)
    zT = np.stack([z.T, zrev.T], axis=0)
    N = 2 * L
    nF = L // 128 + 1
    fm, _ = _freq_map(L)
    wf = np.where(fm < 0, 0.0, np.where((fm == 0) | (fm == L), 1.0, 2.0)) / N
    wf = wf.reshape(nF, 128).T
    return (np.ascontiguousarray(zT, np.float32), np.ascontiguousarray(negt, np.float32),
            np.ascontiguousarray(wf, np.float32))


def _rope_consts():
    n = LX
    row = np.repeat(np.arange(n // 64, dtype=np.float32), 64)
    col = np.tile(np.arange(64, dtype=np.float32), n // 64)
    nf = 16
    inv = (np.float32(10000.0) ** (-np.arange(nf, dtype=np.float32) / np.float32(nf))).astype(np.float32)
    cosT = np.zeros((64, n), np.float32)
    sinT = np.zeros((64, n), np.float32)
    for d in range(64):
        pos = row if d < 32 else col
        ang = (pos * inv[d % 16]).astype(np.float32)
        cosT[d] = np.cos(ang)
        sinT[d] = np.sin(ang)
    Rm = np.zeros((64, 64), np.float32)
    for dp in range(64):
        if (dp % 32) < 16:
            Rm[dp + 16, dp] = -1.0
        else:
            Rm[dp - 16, dp] = 1.0
    return cosT, sinT, Rm


def _cols(v, nchunk):
    return np.ascontiguousarray(np.asarray(v, np.float32).reshape(nchunk, 128).T)


def build_program(stop=None, taps=()):
    nc = bass.Bass("TRN2", target_bir_lowering=False)
    S = Sched(nc)
    pe, act, dve, pool, sp = S.pe, S.act, S.dve, S.pool, S.sp
    taps = set(taps)

    def din(name, shape, dt=F32):
        return nc.dram_tensor(name, list(shape), dt, kind="ExternalInput").ap()

    def dscr(name, shape, dt):
        return nc.dram_tensor(name, list(shape), dt, kind="Internal").ap()

    x_in = din("x", [LX, D])
    ctx_in = din("ctx", [LC, D])
    cT_in = din("cT", [128, 16, 2])
    ada_w = din("ada_w", [DEPTH, D, 3 * D])
    ada_b = din("ada_b", [DEPTH, 3 * D])
    w_in = din("w_in", [DEPTH, D, NIN])
    w_uq = din("w_uq", [DEPTH, 512, 1536])
    w_ukv = din("w_ukv", [DEPTH, 256, 2048])
    w_out = din("w_out", [DEPTH, D, D])
    post_g = din("post_g", [DEPTH, D])
    vec_in = din("vecs", [128, DEPTH, 160])
    filt_w1 = din("filt_w1", [33, DEPTH, 64])
    filt_w2 = din("filt_w2", [64, DEPTH, 64])
    filt_w3 = din("filt_w3", [64, DEPTH, 2048])
    filt_v = din("filt_v", [64, DEPTH, 3])
    ropeC = din("ropeC", [128, LX])
    ropeS = din("ropeS", [128, LX])
    Rm_in = din("Rm", [128, 128])
    absd_in = din("absd", [1, 1024])
    ident_in = din("ident", [128, 128], BF16)
    sgn_in = din("sgn", [128, 1])
    LSX = LX // HB
    NFX = LSX // 128 + 1
    tabF = {LSX: din("tabF_x", [LSX, NFX, 2, 128], BF16), LC: din("tabF_c", [LC, 3, 2, 128], BF16)}
    tabT = {LSX: din("tabT_x", [NFX * 128, 2, LSX], BF16), LC: din("tabT_c", [384, 2, LC], BF16)}
    zT_in = {LX: din("zT_x", [2, 33, LX]), LC: din("zT_c", [2, 33, LC])}
    negt_in = {LX: din("negt_x", [128, 2, LX // 128]), LC: din("negt_c", [128, 2, LC // 128])}
    wf_in = {LSX: din("wf_x", [128, NFX]), LC: din("wf_c", [128, LC // 128 + 1])}
    out_x = nc.dram_tensor("out", [LX, D], F32, kind="ExternalOutput").ap()
    tap_out = {}

    def tap(name, shape, dt=F32):
        tap_out[name] = nc.dram_tensor("tap_" + name, list(shape), dt, kind="ExternalOutput").ap()
        return tap_out[name]

    win_bL = [dscr("win_b%d" % i, [D, NIN], BF16) for i in range(DEPTH)]
    wout_bL = [dscr("wout_b%d" % i, [D, D], BF16) for i in range(DEPTH)]
    wuq_bL = [dscr("wuq_b%d" % i, [512, 1536], BF16) for i in range(DEPTH)]
    wukv_bL = [dscr("wukv_b%d" % i, [256, 2048], BF16) for i in range(DEPTH)]

    def prep_tiles(lp):
        tl = []
        for k in range(16):
            for j in range(3):
                tl.append((w_in[lp], win_bL[lp], k, j * 1984, (j + 1) * 1984, None))
        for k in range(16):
            tl.append((w_out[lp], wout_bL[lp], k, 0, 2048, (lp, 22 + k)))
        for k in range(4):
            tl.append((w_uq[lp], wuq_bL[lp], k, 0, 1536, (lp, 16 + k)))
        for k in range(2):
            tl.append((w_ukv[lp], wukv_bL[lp], k, 0, 2048, (lp, 20 + k)))
        return tl
    modD = dscr("modD", [2, 3 * D], F32)
    hyD = dscr("hyD", [3072, T], BF16)
    gmD = dscr("gmD", [T, 1024], BF16)
    ghD = dscr("ghD", [1024, T], BF16)
    oD = dscr("oD", [T, 1024], F32)
    ymD = dscr("ymD", [1024, T], BF16)
    yhD = dscr("yhD", [1024, T], BF16)
    x0D = dscr("x0D", [1024, LX], F32)
    uD = dscr("uD", [1024, LX], F32)
    KreD = dscr("KreD", [8192, 1024], BF16)
    KimD = dscr("KimD", [8192, 1024], BF16)
    x1D = dscr("x1D", [LX, D], F32)
    ctx1D = dscr("ctx1D", [LC, D], F32)

    dres = {}

    def DR(*key):
        if key not in dres:
            dres[key] = Res()
        return dres[key]

    def sb(name, shape, dt):
        return nc.alloc_sbuf_tensor("g_" + name, list(shape), dt)

    ident = sb("ident", [128, 128], BF16)
    ones_bf = sb("ones_bf", [128, 128], BF16)
    vecs = sb("vecs", [128, DEPTH, 160], F32)
    Rm_sb = sb("Rm_sb", [128, 128], F32)
    sgn = sb("sgn", [128, 1], F32)
    rmAll = sb("rmAll", [128, 34], F32)
    sshAll = sb("sshAll", [128, 34], F32)
    rhAll = sb("rhAll", [128, 34], F32)
    modc = sb("modc", [128, 4, 16], F32)
    axc = sb("axc", [128, 2, 16], F32)
    gpx = sb("gpx", [128, D], F32)
    gpc = sb("gpc", [128, D], F32)
    r_const = Res()
    r_mod = Res()
    r_gp = Res()
    r_rm = Res()
    r_ssh = Res()

    psb = [nc.alloc_psum_tensor("psb%d" % i, [128, 512], F32) for i in range(8)]
    psr = [Res() for _ in range(8)]
    psn = [0]
    ps_set = [list(range(8))]

    def ps_next():
        psn[0] = (psn[0] + 1) % len(ps_set[0])
        i = ps_set[0][psn[0]]
        return psb[i], psr[i]

    rr = [0]

    def ev_eng():
        rr[0] ^= 1
        return act if rr[0] else dve

    def mm(out, lhsT, rhs, start, stop, reads, writes, inc=None):
        S.op(pe, lambda: nc.tensor.matmul(out, lhsT=lhsT, rhs=rhs, start=start, stop=stop),
             reads, writes, inc=(stop if inc is None else inc))

    def copy_on(E, out, in_, reads, writes, scale=None):
        if E is act:
            if scale is None:
                S.op(act, lambda: nc.scalar.copy(out=out, in_=in_), reads, writes)
            else:
                S.op(act, lambda: nc.scalar.activation(out=out, in_=in_, func=AF.Copy, scale=scale), reads, writes)
        else:
            if scale is None:
                S.op(E, lambda: E.e.tensor_copy(out=out, in_=in_), reads, writes)
            else:
                S.op(E, lambda: E.e.tensor_scalar(out=out, in0=in_, scalar1=scale, scalar2=None, op0=ALU.mult),
                     reads, writes)

    def rsqrt_col(dst, src, n, reads, writes, tmp):
        S.op(act, lambda: nc.scalar.activation(out=tmp, in_=src, func=AF.Sqrt, scale=1.0 / n, bias=EPS),
             reads, writes)
        S.op(dve, lambda: nc.vector.reciprocal(out=dst, in_=tmp), writes, writes)

    S.dma(sp, ident[:], ident_in[:, :], [], [r_const])
    S.dma(sp, vecs[:], vec_in[:, :, :], [], [r_const])
    S.dma(sp, Rm_sb[:], Rm_in[:, :], [], [r_const])
    S.dma(sp, sgn[:], sgn_in[:, :], [], [r_const])
    S.op(pool, lambda: nc.gpsimd.memset(ones_bf[:], 1.0), [], [r_const])

    V_PRE, V_QG, V_KVG, V_GCAT, V_CW, V_CB, V_HD = 0, 16, 20, 22, 38, 110, 134

    for l in range(DEPTH):
        last = (l == DEPTH - 1)
        xsrc = x_in if l == 0 else x1D
        csrc = ctx_in if l == 0 else ctx1D
        xdst = out_x if last else x1D

        win_b, wout_b, wuq_b, wukv_b = win_bL[l], wout_bL[l], wuq_bL[l], wukv_bL[l]
        if l == 0:
            with nc.sbuf_tensor("L%d_" % l + "wst", [128, 3, 2048], F32) as wst, nc.sbuf_tensor("L%d_" % l + "wcv", [128, 3, 2048], BF16) as wcv:
                wst_r = [Res() for _ in range(3)]
                wcv_r = [Res() for _ in range(3)]
                for ti, (src, dst, rows, c0, c1, sc_) in enumerate(prep_tiles(0)):
                    i = ti % 3
                    n = c1 - c0
                    scale = None if sc_ is None else vecs[:, sc_[0], sc_[1]:sc_[1] + 1]
                    S.dma(sp, wst[:, i, 0:n], src[rows * 128:(rows + 1) * 128, c0:c1], [], [wst_r[i]])
                    E = [act, dve, pool][ti % 3]
                    copy_on(E, wcv[:, i, 0:n], wst[:, i, 0:n], [wst_r[i], r_const], [wcv_r[i]], scale)
                    S.dma(pool, dst[rows * 128:(rows + 1) * 128, c0:c1], wcv[:, i, 0:n], [wcv_r[i]], [DR("wprep", 0)])
                S.barrier()

        with nc.sbuf_tensor("L%d_" % l + "siluc", [128, 16, 2], F32) as siluc, \
                nc.sbuf_tensor("L%d_" % l + "adw", [128, 2, 16, 256], F32) as adw, \
                nc.sbuf_tensor("L%d_" % l + "adb", [2, 3 * D], F32) as adb, \
                nc.sbuf_tensor("L%d_" % l + "modsb", [2, 3 * D], F32) as modsb, \
                nc.sbuf_tensor("L%d_" % l + "pgb", [128, D], F32) as pgb:
            r_s, r_adb, r_ms, r_pg = Res(), Res(), Res(), Res()
            adw_r = [Res(), Res()]
            S.dma(sp, siluc[:], cT_in[:, :, :], [], [r_s])
            S.op(act, lambda: nc.scalar.activation(out=siluc[:], in_=siluc[:], func=AF.Silu), [r_s], [r_s])
            S.dma(sp, adb[:], ada_b[l:l + 1, :].partition_broadcast(2), [], [r_adb])
            for cb in range(24):
                i = cb % 2
                S.dma(sp, adw[:, i], ada_w[l, :, cb * 256:(cb + 1) * 256].rearrange("(k p) c -> p k c", p=128),
                      [], [adw_r[i]])
                pt, pr = ps_next()
                for k in range(16):
                    mm(pt[0:2, 0:256], siluc[:, k, :], adw[:, i, k, :], k == 0, k == 15, [r_s, adw_r[i]], [pr])
                S.op(dve, lambda: nc.vector.tensor_tensor(out=modsb[:, cb * 256:(cb + 1) * 256], in0=pt[0:2, 0:256],
                                                          in1=adb[:, cb * 256:(cb + 1) * 256], op=ALU.add),
                     [pr, r_adb], [r_ms])
            S.dma(sp, modD[:, :], modsb[:], [r_ms], [DR("mod")])
            for r in range(2):
                for j in range(2):
                    S.dma(sp, modc[:, 2 * r + j, :], modD[r, j * D:(j + 1) * D].rearrange("(k p) -> p k", p=128),
                          [DR("mod")], [r_mod], allow_slow_non_contiguous=True)
            S.dma(sp, gpx[:], modD[0:1, 2 * D:3 * D].partition_broadcast(128), [DR("mod")], [r_gp])
            S.dma(sp, gpc[:], modD[1:2, 2 * D:3 * D].partition_broadcast(128), [DR("mod")], [r_gp])
            S.dma(sp, pgb[:], post_g[l:l + 1, :].partition_broadcast(128), [], [r_pg])
            for r in range(2):
                S.op(dve, lambda: nc.vector.scalar_tensor_tensor(
                    out=axc[:, r, :], in0=modc[:, 2 * r + 1, :], scalar=1.0, in1=vecs[:, l, V_PRE:V_PRE + 16],
                    op0=ALU.add, op1=ALU.mult), [r_mod, r_const], [r_mod])
            S.op(dve, lambda: nc.vector.tensor_tensor(out=gpx[:], in0=gpx[:], in1=pgb[:], op=ALU.mult), [r_gp, r_pg], [r_gp])
            S.op(pool, lambda: nc.gpsimd.tensor_tensor(out=gpc[:], in0=gpc[:], in1=pgb[:], op=ALU.mult), [r_gp, r_pg], [r_gp])
            S.barrier()
        if "mod" in taps and l == 0:
            S.dma(sp, tap("mod", [2, 3 * D])[:, :], modD[:, :], [DR("mod")], [DR("tapmod")])
        if stop == "mod":
            break

        with nc.sbuf_tensor("L%d_" % l + "pqT", [128, 4, T], BF16) as pqT, \
                nc.sbuf_tensor("L%d_" % l + "ckvT", [128, 2, T], BF16) as ckvT, \
                nc.sbuf_tensor("L%d_" % l + "krT", [128, T], BF16) as krT:
            groups = [(0, LC)] + [(LC + 512 * g, 512) for g in range(8)]
            r_pq = [Res() for _ in groups]
            r_ckv = [Res() for _ in groups]
            r_kr = [Res() for _ in groups]

            with nc.sbuf_tensor("L%d_" % l + "xt", [128, 2, D], F32) as xt, \
                    nc.sbuf_tensor("L%d_" % l + "xn", [128, 4, D], BF16) as xn, \
                    nc.sbuf_tensor("L%d_" % l + "stat", [128, 8], F32) as stat, \
                    nc.sbuf_tensor("L%d_" % l + "hxT", [128, 2, 16, 512], BF16) as hxT, \
                    nc.sbuf_tensor("L%d_" % l + "wblk", [128, 2, 16, 512], BF16) as wblk, \
                    nc.sbuf_tensor("L%d_" % l + "ost", [128, 4, 512], BF16) as ost, \
                    nc.sbuf_tensor("L%d_" % l + "qraw", [128, 4, 512], F32) as qraw, \
                    nc.sbuf_tensor("L%d_" % l + "qsq", [128, 4, 512], BF16) as qsq, \
                    nc.sbuf_tensor("L%d_" % l + "rbc", [128, 512], F32) as rbc, \
                    nc.sbuf_tensor("L%d_" % l + "krw", [128, 4, 512], F32) as krw, \
                    nc.sbuf_tensor("L%d_" % l + "wkr2", [128, 16, 128], BF16) as wkr2:
                xt_r = [Res(), Res()]
                xn_r = [Res() for _ in range(4)]
                r_sqj, r_stat = Res(), Res()
                r_wkr = Res()
                for dpl in range(2):
                    S.dma(sp, wkr2[:, :, dpl * 64:(dpl + 1) * 64], win_b[:, OFF_KR:OFF_KR + 64].rearrange("(k p) c -> p k c", p=128),
                          [DR("wprep", l)], [r_wkr])
                hx_r = [Res(), Res()]
                wb_r = [Res(), Res()]
                ost_r = [Res() for _ in range(4)]
                r_qraw, r_qsq, r_rbc, r_krw = Res(), Res(), Res(), Res()
                wbc = [0]
                tcount = [0]
                wblocks = [(0, 512, "q"), (512, 320, "kv"), (832, 512, "gm"), (1344, 512, "gm")]
                wblocks += [(OFF_HY + 512 * j, 512, "hy") for j in range(6)]
                wblocks += [(OFF_GH + 512 * j, 512, "gh") for j in range(2)]
                def hx_part1(gi):
                    t0, n = groups[gi]
                    isx = t0 >= LC
                    for tt in range(n // 128):
                        i = tcount[0] % 2
                        tcount[0] += 1
                        src = (xsrc[t0 - LC + tt * 128:t0 - LC + (tt + 1) * 128, :] if isx
                               else csrc[t0 + tt * 128:t0 + (tt + 1) * 128, :])
                        rsrc = DR("xres", (t0 + tt * 128) // 128)
                        S.dma(sp, xt[:, i, :], src, [rsrc], [xt_r[i]])
                        S.op(pool, lambda: nc.gpsimd.memset(stat[:, 0:1], 0.0), [], [r_stat])
                        S.op(act, lambda: nc.scalar.activation(out=xn[:, tt, :], in_=xt[:, i, :], func=AF.Square,
                                                               accum_out=stat[:, 0:1]), [xt_r[i]], [xn_r[tt], r_stat])
                        rsqrt_col(stat[:, 2:3], stat[:, 0:1], D, [r_stat], [r_stat], stat[:, 1:2])
                        S.op(act, lambda: nc.scalar.activation(out=xn[:, tt, :], in_=xt[:, i, :], func=AF.Copy,
                                                               scale=stat[:, 2:3]), [xt_r[i], r_stat], [xn_r[tt]])

                def hx_part2(gi):
                    t0, n = groups[gi]
                    isx = t0 >= LC
                    mi = 0 if isx else 1
                    hb = gi % 2
                    for tt in range(n // 128):
                        for kq in range(4):
                            pt, pr = ps_next()
                            ptb = pt[:].bitcast(BF16)
                            for kk in range(4):
                                k = kq * 4 + kk
                                S.op(pe, lambda: nc.tensor.transpose(ptb[:, kk * 128:(kk + 1) * 128],
                                                                     xn[:, tt, k * 128:(k + 1) * 128], ident[:]),
                                     [xn_r[tt], r_const], [pr], inc=(kk == 3))
                            for kk in range(4):
                                k = kq * 4 + kk
                                E = ev_eng()
                                o_ap = hxT[:, hb, k, tt * 128:(tt + 1) * 128]
                                i_ap = ptb[:, kk * 128:(kk + 1) * 128]
                                a_ap = axc[:, mi, k:k + 1]
                                s_ap = modc[:, 2 * mi, k:k + 1]
                                if E is act:
                                    S.op(act, lambda: nc.scalar.activation(out=o_ap, in_=i_ap, func=AF.Identity,
                                                                           scale=a_ap, bias=s_ap),
                                         [pr, r_mod], [hx_r[hb]])
                                else:
                                    S.op(dve, lambda: nc.vector.tensor_scalar(out=o_ap, in0=i_ap, scalar1=a_ap,
                                                                              scalar2=s_ap, op0=ALU.mult, op1=ALU.add),
                                         [pr, r_mod], [hx_r[hb]])

                hx_part1(0)
                hx_part2(0)
                for gi, (t0, n) in enumerate(groups):
                    isx = t0 >= LC
                    hb = gi % 2
                    for bi, (c0, ncol, kind) in enumerate(wblocks):
                        if gi + 1 < len(groups):
                            if bi == 2:
                                hx_part1(gi + 1)
                            if bi == 8:
                                hx_part2(gi + 1)
                        wi = wbc[0] % 2
                        wbc[0] += 1
                        S.dma(sp, wblk[:, wi, :, 0:ncol],
                              win_b[:, c0:c0 + ncol].rearrange("(k p) c -> p k c", p=128),
                              [DR("wprep", l)], [wb_r[wi]])
                        if kind in ("q", "kv"):
                            nblk = 4 if kind == "q" else 2
                            dim = 512.0 if kind == "q" else 256.0
                            pss = []
                            for m in range(nblk):
                                pt, pr = ps_next()
                                for k in range(16):
                                    mm(pt[:, 0:n], wblk[:, wi, k, m * 128:(m + 1) * 128], hxT[:, hb, k, 0:n],
                                       k == 0, k == 15, [wb_r[wi], hx_r[hb]], [pr])
                                pss.append((pt, pr))
                            for m, (pt, pr) in enumerate(pss):
                                S.op(act, lambda: nc.scalar.copy(out=qraw[:, m, 0:n], in_=pt[:, 0:n]), [pr], [r_qraw])
                                S.op(act, lambda: nc.scalar.activation(out=qsq[:, m, 0:n], in_=pt[:, 0:n], func=AF.Square),
                                     [pr], [r_qsq])
                            pt, pr = ps_next()
                            for m in range(nblk):
                                mm(pt[:, 0:n], ones_bf[:], qsq[:, m, 0:n], m == 0, m == nblk - 1, [r_qsq, r_const], [pr])
                            S.op(act, lambda: nc.scalar.activation(out=rbc[:, 0:n], in_=pt[:, 0:n], func=AF.Sqrt,
                                                                   scale=1.0 / dim, bias=EPS), [pr], [r_rbc])
                            S.op(dve, lambda: nc.vector.reciprocal(out=rbc[:, 0:n], in_=rbc[:, 0:n]), [r_rbc], [r_rbc])
                            dstT, dres_ = (pqT, r_pq[gi]) if kind == "q" else (ckvT, r_ckv[gi])
                            for m in range(nblk):
                                E = dve if m % 2 == 0 else pool
                                S.op(E, lambda: E.e.tensor_tensor(out=dstT[:, m, t0:t0 + n], in0=qraw[:, m, 0:n],
                                                                  in1=rbc[:, 0:n], op=ALU.mult),
                                     [r_qraw, r_rbc], [dres_])
                            if kind == "kv":
                                pt, pr = ps_next()
                                for k in range(16):
                                    mm(pt[:, 0:n], wkr2[:, k, :], hxT[:, hb, k, 0:n], k == 0, k == 15,
                                       [r_wkr, hx_r[hb]], [pr])
                                if not isx:
                                    S.op(act, lambda: nc.scalar.copy(out=krT[:, t0:t0 + n], in_=pt[:, 0:n]), [pr], [r_kr[gi]])
                                else:
                                    p0 = t0 - LC
                                    S.op(act, lambda: nc.scalar.copy(out=krw[:, 0, 0:n], in_=pt[:, 0:n]), [pr], [r_krw])
                                    S.dma(sp, krw[:, 1, 0:n], ropeC[:, p0:p0 + n], [], [r_krw])
                                    S.dma(sp, krw[:, 2, 0:n], ropeS[:, p0:p0 + n], [], [r_krw])
                                    pt2, pr2 = ps_next()
                                    mm(pt2[:, 0:n], Rm_sb[:], krw[:, 0, 0:n], True, True, [r_krw, r_const], [pr2])
                                    S.op(dve, lambda: nc.vector.tensor_tensor(out=krw[:, 3, 0:n], in0=pt2[:, 0:n],
                                                                              in1=krw[:, 2, 0:n], op=ALU.mult), [pr2, r_krw], [r_krw])
                                    S.op(pool, lambda: nc.gpsimd.tensor_tensor(out=krw[:, 1, 0:n], in0=krw[:, 0, 0:n],
                                                                               in1=krw[:, 1, 0:n], op=ALU.mult), [r_krw], [r_krw])
                                    S.op(dve, lambda: nc.vector.tensor_tensor(out=krT[:, t0:t0 + n], in0=krw[:, 1, 0:n],
                                                                              in1=krw[:, 3, 0:n], op=ALU.add), [r_krw], [r_kr[gi]])
                        elif kind == "gm":
                            cg = (c0 - OFF_GM) // 512
                            for tt in range(n // 128):
                                pt, pr = ps_next()
                                for k in range(16):
                                    mm(pt[:, :], hxT[:, hb, k, tt * 128:(tt + 1) * 128], wblk[:, wi, k, :], k == 0, k == 15,
                                       [wb_r[wi], hx_r[hb]], [pr])
                                oi = tt % 4
                                S.op(act, lambda: nc.scalar.activation(out=ost[:, oi, :], in_=pt[:, :], func=AF.Silu),
                                     [pr], [ost_r[oi]])
                                S.dma(pool, gmD[t0 + tt * 128:t0 + (tt + 1) * 128, cg * 512:(cg + 1) * 512], ost[:, oi, :],
                                      [ost_r[oi]], [DR("gm", (t0 + tt * 128) // 128)])
                        else:
                            for m in range(4):
                                pt, pr = ps_next()
                                for k in range(16):
                                    mm(pt[:, 0:n], wblk[:, wi, k, m * 128:(m + 1) * 128], hxT[:, hb, k, 0:n], k == 0, k == 15,
                                       [wb_r[wi], hx_r[hb]], [pr])
                                oi = m
                                if kind == "hy":
                                    E = ev_eng()
                                    copy_on(E, ost[:, oi, 0:n], pt[:, 0:n], [pr], [ost_r[oi]])
                                    row0 = (c0 - OFF_HY) + m * 128
                                    S.dma(pool, hyD[row0:row0 + 128, t0:t0 + n], ost[:, oi, 0:n], [ost_r[oi]],
                                          [DR("hy", row0 // 128, gi)])
                                else:
                                    S.op(act, lambda: nc.scalar.activation(out=ost[:, oi, 0:n], in_=pt[:, 0:n], func=AF.Silu),
                                         [pr], [ost_r[oi]])
                                    row0 = (c0 - OFF_GH) + m * 128
                                    S.dma(pool, ghD[row0:row0 + 128, t0:t0 + n], ost[:, oi, 0:n], [ost_r[oi]],
                                          [DR("gh", row0 // 128, gi)])
                S.barrier()
            if "A" in taps and l == 0:
                with nc.sbuf_tensor("L%d_" % l + "tapst", [128, T], F32) as tapst:
                    r_t = Res()
                    tq = tap("pq", [4, 128, T])
                    tk = tap("ckv", [2, 128, T])
                    tr = tap("kr", [64, T])
                    for m in range(4):
                        S.op(dve, lambda: nc.vector.tensor_copy(out=tapst[:], in_=pqT[:, m, :]), r_pq, [r_t])
                        S.dma(sp, tq[m], tapst[:], [r_t], [DR("tapq", m)])
                    for m in range(2):
                        S.op(dve, lambda: nc.vector.tensor_copy(out=tapst[:], in_=ckvT[:, m, :]), r_ckv, [r_t])
                        S.dma(sp, tk[m], tapst[:], [r_t], [DR("tapk", m)])
                    S.op(dve, lambda: nc.vector.tensor_copy(out=tapst[0:64, :], in_=krT[0:64, :]), r_kr, [r_t])
                    S.dma(sp, tr[:, :], tapst[0:64, :], [r_t], [DR("tapr")])
                    S.dma(sp, tap("hy", [3072, T], BF16)[:, :], hyD[:, :], [DR("hy", a, b) for a in range(24) for b in range(9)], [DR("taphy")])
                    S.dma(sp, tap("gm", [T, 1024], BF16)[:, :], gmD[:, :], [DR("gm", a) for a in range(34)], [DR("tapgm")])
                    S.dma(sp, tap("gh", [1024, T], BF16)[:, :], ghD[:, :], [DR("gh", a, b) for a in range(8) for b in range(9)], [DR("tapgh")])
                    S.barrier()
            if stop == "A":
                break

            do_ctx_q = not last
            with nc.sbuf_tensor("L%d_" % l + "wuq", [128, 4, 1536], BF16) as wuq_s, \
                    nc.sbuf_tensor("L%d_" % l + "wukv", [128, 2, 2048], BF16) as wukv_s, \
                    nc.sbuf_tensor("L%d_" % l + "KhT", [128, 2, T], BF16) as KhT, \
                    nc.sbuf_tensor("L%d_" % l + "Vh", [128, 2, 34, 132], BF16) as Vh, \
                    nc.sbuf_tensor("L%d_" % l + "QnT", [128, 2, 512], BF16) as QnT, \
                    nc.sbuf_tensor("L%d_" % l + "QrT", [128, 2, 512], BF16) as QrT, \
                    nc.sbuf_tensor("L%d_" % l + "qrw", [128, 4, 512], F32) as qrw, \
                    nc.sbuf_tensor("L%d_" % l + "wqr2", [128, H, 4, 128], BF16) as wqr2, \
                    nc.sbuf_tensor("L%d_" % l + "wst2", [128, 3, 2048], F32) as wst2, \
                    nc.sbuf_tensor("L%d_" % l + "wcv2", [128, 3, 2048], BF16) as wcv2, \
                    nc.sbuf_tensor("L%d_" % l + "PT", [128, 8, 512], BF16) as PT, \
                    nc.sbuf_tensor("L%d_" % l + "osb", [128, 8, 128], F32) as osb, \
                    nc.sbuf_tensor("L%d_" % l + "rec", [128, 8], F32) as rec:
                r_w, r_qrw, r_rec = Res(), Res(), Res()
                wst2_r = [Res() for _ in range(3)]
                wcv2_r = [Res() for _ in range(3)]
                r_K = [Res(), Res()]
                r_V = [Res(), Res()]
                qn_r = [Res(), Res()]
                qr_r = [Res(), Res()]
                pt_r = [Res() for _ in range(8)]
                osb_r = [Res() for _ in range(8)]
                S.dma(sp, wuq_s[:], wuq_b[:, :].rearrange("(k p) c -> p k c", p=128), [DR("wprep", l)], [r_w])
                S.dma(sp, wukv_s[:], wukv_b[:, :].rearrange("(k p) c -> p k c", p=128), [DR("wprep", l)], [r_w])
                for kb_ in range(2):
                    S.op(pool, lambda: nc.gpsimd.memset(Vh[:, kb_, :, 128:129], 1.0), [], [r_V[kb_]])
                for hh in range(H):
                    for dpl in range(2):
                        S.dma(sp, wqr2[:, hh, :, dpl * 64:(dpl + 1) * 64],
                              wuq_b[:, hh * 192 + 128:hh * 192 + 192].rearrange("(k p) c -> p k c", p=128),
                              [DR("wprep", l)], [r_w])
                ps_set[0] = [0, 1, 2, 3]
                accs2 = [[(psb[4], psr[4], 0), (psb[4], psr[4], 256), (psb[5], psr[5], 0), (psb[5], psr[5], 256)],
                         [(psb[6], psr[6], 0), (psb[6], psr[6], 256), (psb[7], psr[7], 0), (psb[7], psr[7], 256)]]
                qcount = [0]
                pcount = [0]

                def grp_of_tile(kt):
                    return 0 if kt < 2 else 1 + (kt * 128 - LC) // 512

                def build_kv(h, kb):
                    for gi, (t0, n) in enumerate(groups):
                        pt, pr = ps_next()
                        for kc in range(2):
                            mm(pt[:, 0:n], wukv_s[:, kc, h * 256:h * 256 + 128], ckvT[:, kc, t0:t0 + n], kc == 0, kc == 1,
                               [r_w, r_ckv[gi]], [pr])
                        copy_on(dve, KhT[:, kb, t0:t0 + n], pt[:, 0:n], [pr], [r_K[kb]])
                    for kt0 in range(0, 34, 4):
                        nb = min(4, 34 - kt0)
                        pt, pr = ps_next()
                        for j in range(nb):
                            kt = kt0 + j
                            for kc in range(2):
                                mm(pt[:, j * 128:(j + 1) * 128], ckvT[:, kc, kt * 128:(kt + 1) * 128],
                                   wukv_s[:, kc, h * 256 + 128:h * 256 + 256], kc == 0, kc == 1,
                                   [r_w, r_ckv[grp_of_tile(kt)]], [pr])
                        copy_on(dve, Vh[:, kb, kt0:kt0 + nb, 0:128],
                                pt[:, 0:nb * 128].rearrange("p (a b) -> p a b", b=128), [pr], [r_V[kb]])

                def build_q(h, qi, qb):
                    t0, n = groups[qi]
                    isx = t0 >= LC
                    pt, pr = ps_next()
                    for kc in range(4):
                        mm(pt[:, 0:n], wuq_s[:, kc, h * 192:h * 192 + 128], pqT[:, kc, t0:t0 + n], kc == 0, kc == 3,
                           [r_w, r_pq[qi]], [pr])
                    copy_on(dve, QnT[:, qb, 0:n], pt[:, 0:n], [pr], [qn_r[qb]], SCALE)
                    pt2, pr2 = ps_next()
                    for kc in range(4):
                        mm(pt2[:, 0:n], wqr2[:, h, kc, :], pqT[:, kc, t0:t0 + n], kc == 0, kc == 3,
                           [r_w, r_pq[qi]], [pr2])
                    if not isx:
                        copy_on(dve, QrT[:, qb, 0:n], pt2[:, 0:n], [pr2], [qr_r[qb]], SCALE)
                    else:
                        p0 = t0 - LC
                        copy_on(dve, qrw[:, 0, 0:n], pt2[:, 0:n], [pr2], [r_qrw], SCALE)
                        S.dma(sp, qrw[:, 1, 0:n], ropeC[:, p0:p0 + n], [], [r_qrw])
                        S.dma(sp, qrw[:, 2, 0:n], ropeS[:, p0:p0 + n], [], [r_qrw])
                        pt3, pr3 = ps_next()
                        mm(pt3[:, 0:n], Rm_sb[:], qrw[:, 0, 0:n], True, True, [r_qrw, r_const], [pr3])
                        S.op(dve, lambda: nc.vector.tensor_tensor(out=qrw[:, 3, 0:n], in0=pt3[:, 0:n],
                                                                  in1=qrw[:, 2, 0:n], op=ALU.mult), [pr3, r_qrw], [r_qrw])
                        S.op(pool, lambda: nc.gpsimd.tensor_tensor(out=qrw[:, 1, 0:n], in0=qrw[:, 0, 0:n],
                                                                   in1=qrw[:, 1, 0:n], op=ALU.mult), [r_qrw], [r_qrw])
                        S.op(dve, lambda: nc.vector.tensor_tensor(out=QrT[:, qb, 0:n], in0=qrw[:, 1, 0:n],
                                                                  in1=qrw[:, 3, 0:n], op=ALU.add), [r_qrw], [qr_r[qb]])

                items = [(h, qi) for h in range(H) for qi, (t0, n) in enumerate(groups) if (t0 >= LC or do_ctx_q)]
                nxt_tiles = prep_tiles(l + 1) if not last else []
                nxt_pending = []
                nxt_cnt = [0]

                def nxt_prep_step(ntile):
                    for (ti, src, dst, rows, c0, c1, sc_) in nxt_pending:
                        i = ti % 3
                        n_ = c1 - c0
                        scale = None if sc_ is None else vecs[:, sc_[0], sc_[1]:sc_[1] + 1]
                        copy_on(pool, wcv2[:, i, 0:n_], wst2[:, i, 0:n_], [wst2_r[i], r_const], [wcv2_r[i]], scale)
                        S.dma(pool, dst[rows * 128:(rows + 1) * 128, c0:c1], wcv2[:, i, 0:n_], [wcv2_r[i]], [DR("wprep", l + 1)])
                    del nxt_pending[:]
                    for _ in range(ntile):
                        if nxt_cnt[0] >= len(nxt_tiles):
                            break
                        ti = nxt_cnt[0]
                        nxt_cnt[0] += 1
                        (src, dst, rows, c0, c1, sc_) = nxt_tiles[ti]
                        i = ti % 3
                        S.dma(sp, wst2[:, i, 0:c1 - c0], src[rows * 128:(rows + 1) * 128, c0:c1], [], [wst2_r[i]])
                        nxt_pending.append((ti, src, dst, rows, c0, c1, sc_))

                build_kv(0, 0)
                build_q(items[0][0], items[0][1], 0)
                for idx, (h, qi) in enumerate(items):
                    t0, n = groups[qi]
                    isx = t0 >= LC
                    kts = list(range(34)) if isx else [0, 1]
                    qb = idx % 2
                    kb = h % 2
                    acc = accs2[qb]
                    if nxt_tiles:
                        nxt_prep_step(1 if idx % 4 else 2)
                    if idx + 1 < len(items):
                        nh, nqi = items[idx + 1]
                        if nh != h:
                            build_kv(nh, nh % 2)
                        build_q(nh, nqi, (idx + 1) % 2)
                    nq = n // 128

                    def pv(ki, kt, pb):
                        for j in range(nq):
                            at, ar, c0 = acc[j]
                            mm(at[:, c0:c0 + 129], PT[:, pb, j * 128:(j + 1) * 128], Vh[:, kb, kt, 0:129],
                               ki == 0 and c0 == 0, ki == len(kts) - 1, [pt_r[pb], r_V[kb]], [ar])

                    pend = []
                    for ki0 in range(0, len(kts), 2):
                        pair = []
                        for dk in range(2):
                            kt = kts[ki0 + dk]
                            pt, pr = ps_next()
                            mm(pt[:, 0:n], KhT[:, kb, kt * 128:(kt + 1) * 128], QnT[:, qb, 0:n], True, False, [r_K[kb], qn_r[qb]], [pr])
                            pair.append((kt, pt, pr))
                        for dk, (kt, pt, pr) in enumerate(pair):
                            r0 = dk * 64
                            mm(pt[:, 0:n], krT[r0:r0 + 64, kt * 128:(kt + 1) * 128], QrT[r0:r0 + 64, qb, 0:n], False, True,
                               [r_kr[grp_of_tile(kt)], qr_r[qb]], [pr])
                        for dk, (kt, pt, pr) in enumerate(pair):
                            pb = pcount[0] % 8
                            pcount[0] += 1
                            S.op(act, lambda: nc.scalar.activation(out=PT[:, pb, 0:n], in_=pt[:, 0:n], func=AF.Exp),
                                 [pr], [pt_r[pb]])
                            pend.append((ki0 + dk, kt, pb))
                        while len(pend) > PIPE_DEPTH:
                            pv(*pend.pop(0))
                    while pend:
                        pv(*pend.pop(0))
                    for j in range(nq):
                        at, ar, c0 = acc[j]
                        oj = (idx % 2) * 4 + j
                        S.op(dve, lambda: nc.vector.reciprocal(out=rec[:, oj:oj + 1], in_=at[:, c0 + 128:c0 + 129]), [ar], [r_rec])
                        S.op(dve, lambda: nc.vector.tensor_scalar(out=osb[:, oj, :], in0=at[:, c0:c0 + 128], scalar1=rec[:, oj:oj + 1],
                                                                  scalar2=None, op0=ALU.mult), [ar, r_rec], [osb_r[oj]])
                        S.dma(pool, oD[t0 + j * 128:t0 + (j + 1) * 128, h * 128:(h + 1) * 128], osb[:, oj, :],
                              [osb_r[oj]], [DR("o", (t0 + j * 128) // 128, h)])
                while nxt_tiles and (nxt_pending or nxt_cnt[0] < len(nxt_tiles)):
                    nxt_prep_step(2)
                ps_set[0] = list(range(8))
                S.barrier()
            with nc.sbuf_tensor("L%d_" % l + "ot", [128, 2, 1024], F32) as ot, \
                    nc.sbuf_tensor("L%d_" % l + "gmt", [128, 2, 1024], BF16) as gmt, \
                    nc.sbuf_tensor("L%d_" % l + "ysq", [128, 1024], BF16) as ysq, \
                    nc.sbuf_tensor("L%d_" % l + "ybf", [128, 2, 1024], BF16) as ybf, \
                    nc.sbuf_tensor("L%d_" % l + "ymst", [128, 2, 8, 128], BF16) as ymst, \
                    nc.sbuf_tensor("L%d_" % l + "est", [128, 4], F32) as est:
                ot_r = [Res(), Res()]
                gm_r = [Res(), Res()]
                yb_r = [Res(), Res()]
                ym_r = [Res(), Res()]
                r_ysq, r_est = Res(), Res()
                tiles = list(range(34)) if do_ctx_q else list(range(2, 34))
                for ti, tl in enumerate(tiles):
                    i = ti % 2
                    S.dma(sp, ot[:, i, :], oD[tl * 128:(tl + 1) * 128, :], [DR("o", tl, h) for h in range(H)], [ot_r[i]])
                    S.dma(sp, gmt[:, i, :], gmD[tl * 128:(tl + 1) * 128, :], [DR("gm", tl)], [gm_r[i]])
                    S.op(pool, lambda: nc.gpsimd.memset(est[:, 0:1], 0.0), [], [r_est])
                    S.op(act, lambda: nc.scalar.activation(out=ysq[:], in_=ot[:, i, :], func=AF.Square, accum_out=est[:, 0:1]),
                         [ot_r[i]], [r_ysq, r_est])
                    S.op(act, lambda: nc.scalar.activation(out=est[:, 1:2], in_=est[:, 0:1], func=AF.Sqrt, scale=1.0 / 1024.0, bias=EPS),
                         [r_est], [r_est])
                    S.op(dve, lambda: nc.vector.reciprocal(out=rmAll[:, tl:tl + 1], in_=est[:, 1:2]), [r_est], [r_rm])
                    S.op(dve, lambda: nc.vector.tensor_tensor(out=ybf[:, i, :], in0=ot[:, i, :], in1=gmt[:, i, :], op=ALU.mult),
                         [ot_r[i], gm_r[i]], [yb_r[i]])
                    for kq in range(2):
                        pt, pr = ps_next()
                        ptb = pt[:].bitcast(BF16)
                        for kk in range(4):
                            k = kq * 4 + kk
                            S.op(pe, lambda: nc.tensor.transpose(ptb[:, kk * 128:(kk + 1) * 128], ybf[:, i, k * 128:(k + 1) * 128], ident[:]),
                                 [yb_r[i], r_const], [pr], inc=(kk == 3))
                        copy_on(ev_eng(), ymst[:, i, kq * 4:(kq + 1) * 4, :],
                                ptb[:, 0:512].rearrange("p (a b) -> p a b", b=128), [pr], [ym_r[i]])
                    S.dma(pool, ymD[:, tl * 128:(tl + 1) * 128].rearrange("(k p) t -> p k t", p=128), ymst[:, i],
                          [ym_r[i]], [DR("ym", tl)])
                S.barrier()
            if "B" in taps and l == 0:
                S.dma(sp, tap("o", [T, 1024])[:, :], oD[:, :], [], [DR("tapo")])
                S.dma(sp, tap("ym", [1024, T], BF16)[:, :], ymD[:, :], [], [DR("tapym")])
                S.dma(sp, tap("rm", [128, 34])[:, :], rmAll[:], [r_rm], [DR("taprm")])
                S.barrier()
            if stop == "B":
                break

        S.op(pool, lambda: nc.gpsimd.memset(sshAll[:], 0.0), [], [r_ssh])

        def hyena_seq(seq_t0, L, tg, B):
            nt = L // 128
            Ls = L // B
            nts = Ls // 128
            nF = nts + 1
            ne_blk = nts // 2 + 1
            pcw = min(512, Ls)
            npc = Ls // pcw
            nD = 2 * B - 1
            tabf, tabt = tabF[Ls], tabT[Ls]
            with nc.sbuf_tensor(tg + "w1s", [33, 64], F32) as w1s, \
                    nc.sbuf_tensor(tg + "w2s", [64, 64], F32) as w2s, \
                    nc.sbuf_tensor(tg + "w3s", [64, 2048], F32) as w3s, \
                    nc.sbuf_tensor(tg + "fv", [64, 3], F32) as fv, \
                    nc.sbuf_tensor(tg + "zt", [33, 2, 512], F32) as zt, \
                    nc.sbuf_tensor(tg + "fa", [64, 2, 512], F32) as fa, \
                    nc.sbuf_tensor(tg + "h1", [64, 512], F32) as h1, \
                    nc.sbuf_tensor(tg + "h2", [64, 2, L], F32) as h2, \
                    nc.sbuf_tensor(tg + "absd", [128, 1024], F32) as absd, \
                    nc.sbuf_tensor(tg + "negt", [128, 2, nt], F32) as negt, \
                    nc.sbuf_tensor(tg + "wfc", [128, nF], F32) as wfc, \
                    nc.sbuf_tensor(tg + "dec", [128, 2, 512], F32) as dec, \
                    nc.sbuf_tensor(tg + "SEG", [128, 2 * nt, 512], BF16) as SEG, \
                    nc.sbuf_tensor(tg + "kAB", [128, 1, 2, nts, 512], BF16) as kAB, \
                    nc.sbuf_tensor(tg + "slab", [128, 2, nts, 2, 128], BF16) as slab, \
                    nc.sbuf_tensor(tg + "kst", [128, 2, 2, 512], BF16) as kst:
                r_fw, r_fa, r_h1, r_h2, r_cst, r_seg = Res(), Res(), Res(), Res(), Res(), Res()
                zt_r = [Res(), Res()]
                dec_r = [Res(), Res()]
                kab_r = [Res(), Res()]
                slab_r = [Res(), Res()]
                kst_r = [Res(), Res()]
                S.dma(sp, w1s[:], filt_w1[:, l, :], [], [r_fw])
                S.dma(sp, w2s[:], filt_w2[:, l, :], [], [r_fw])
                S.dma(sp, w3s[:], filt_w3[:, l, :], [], [r_fw])
                S.dma(sp, fv[:], filt_v[:, l, :], [], [r_fw])
                S.dma(sp, absd[:], absd_in[0:1, :].partition_broadcast(128), [], [r_cst])
                S.dma(sp, negt[:], negt_in[L][:, :, :], [], [r_cst])
                S.dma(sp, wfc[:], wf_in[Ls][:, :], [], [r_cst])
                zc = [0]

                def sin_layer(ps_ap, pr, bcol, out_ap, out_res, n):
                    S.op(dve, lambda: nc.vector.tensor_scalar(out=fa[:, 0, 0:n], in0=ps_ap, scalar1=fv[:, bcol:bcol + 1],
                                                              scalar2=fv[:, 1:2], op0=ALU.add, op1=ALU.mult), [pr, r_fw], [r_fa])
                    S.op(dve, lambda: nc.vector.tensor_scalar(out=fa[:, 1, 0:n], in0=fa[:, 0, 0:n], scalar1=1.0 / TWO_PI,
                                                              scalar2=MAGIC, op0=ALU.mult, op1=ALU.add), [r_fa], [r_fa])
                    S.op(dve, lambda: nc.vector.tensor_scalar(out=fa[:, 1, 0:n], in0=fa[:, 1, 0:n], scalar1=MAGIC,
                                                              scalar2=-TWO_PI, op0=ALU.subtract, op1=ALU.mult), [r_fa], [r_fa])
                    S.op(dve, lambda: nc.vector.tensor_tensor(out=fa[:, 0, 0:n], in0=fa[:, 0, 0:n], in1=fa[:, 1, 0:n], op=ALU.add),
                         [r_fa], [r_fa])
                    S.op(act, lambda: nc.scalar.activation(out=out_ap, in_=fa[:, 0, 0:n], func=AF.Sin), [r_fa], [out_res])

                fpw = min(512, L)
                for dr in range(2):
                    for pc in range(L // fpw):
                        zi = zc[0] % 2
                        zc[0] += 1
                        S.dma(sp, zt[:, zi, 0:fpw], zT_in[L][dr, :, pc * fpw:(pc + 1) * fpw], [], [zt_r[zi]])
                        pt, pr = ps_next()
                        mm(pt[0:64, 0:fpw], w1s[:], zt[:, zi, 0:fpw], True, True, [r_fw, zt_r[zi]], [pr])
                        sin_layer(pt[0:64, 0:fpw], pr, 0, h1[:, 0:fpw], r_h1, fpw)
                        pt, pr = ps_next()
                        mm(pt[0:64, 0:fpw], w2s[:], h1[:, 0:fpw], True, True, [r_fw, r_h1], [pr])
                        sin_layer(pt[0:64, 0:fpw], pr, 2, h2[:, dr, pc * fpw:(pc + 1) * fpw], r_h2, fpw)
                sc = [0]
                kc = [0]
                for hf in range(2):
                    for tt in range(nt):
                        for dr in range(2):
                            pt, pr = ps_next()
                            mm(pt[:, :], h2[:, dr, tt * 128:(tt + 1) * 128], w3s[:, dr * 1024 + hf * 512:dr * 1024 + (hf + 1) * 512],
                               True, True, [r_h2, r_fw], [pr])
                            S.op(act, lambda: nc.scalar.activation(out=dec[:, dr, :], in_=absd[:, hf * 512:(hf + 1) * 512], func=AF.Exp,
                                                                   scale=negt[:, dr, tt:tt + 1]), [r_cst], [dec_r[dr]])
                            S.op(dve, lambda: nc.vector.tensor_tensor(out=SEG[:, (1 - dr) * nt + tt, :], in0=pt[:, :], in1=dec[:, dr, :],
                                                                      op=ALU.mult), [pr, dec_r[dr]], [r_seg])
                    for dd in range(nD):
                        d = dd - (B - 1)
                        e1 = (d + B) * nts
                        e0 = (d - 1 + B) * nts
                        ki = 0
                        kc[0] += 1
                        S.op(pool, lambda: nc.gpsimd.tensor_tensor(out=kAB[:, ki, 0], in0=SEG[:, e1:e1 + nts, :], in1=SEG[:, e0:e0 + nts, :],
                                                                   op=ALU.add), [r_seg], [kab_r[ki]])
                        S.op(dve, lambda: nc.vector.tensor_tensor(out=kAB[:, ki, 1], in0=SEG[:, e1:e1 + nts, :], in1=SEG[:, e0:e0 + nts, :],
                                                                  op=ALU.subtract), [r_seg], [kab_r[ki]])
                        for j in range(nF):
                            si = sc[0] % 2
                            sc[0] += 1
                            par = 0 if j < ne_blk else 1
                            S.dma(sp, slab[:, si], tabf[0:Ls, j].rearrange("(i p) c f -> p i c f", p=128), [], [slab_r[si]])
                            accs = [ps_next() for _ in range(2)]
                            for cs in range(2):
                                for i in range(nts):
                                    mm(accs[cs][0][:, :], slab[:, si, i, cs, :], kAB[:, ki, par, i, :], i == 0, i == nts - 1,
                                       [slab_r[si], kab_r[ki]], [accs[cs][1]])
                            for cs in range(2):
                                S.op(act, lambda: nc.scalar.activation(out=kst[:, si, cs, :], in_=accs[cs][0][:, :], func=AF.Copy,
                                                                       scale=wfc[:, j:j + 1]), [accs[cs][1], r_cst], [kst_r[si]])
                            r0 = (dd * nF + j) * 128
                            S.dma(pool, KreD[r0:r0 + 128, hf * 512:(hf + 1) * 512], kst[:, si, 0, :], [kst_r[si]], [DR("Kf", dd, j, hf, 0)])
                            S.dma(pool, KimD[r0:r0 + 128, hf * 512:(hf + 1) * 512], kst[:, si, 1, :], [kst_r[si]], [DR("Kf", dd, j, hf, 1)])
                S.barrier()
            if DBG.get("cstage") == 0:
                return
            with nc.sbuf_tensor(tg + "uTM", [128, nt, 512], BF16) as uTM, \
                    nc.sbuf_tensor(tg + "Y", [128, B, 2, nF, 512], BF16) as Y:
                for hf in range(2):
                    r_u = Res()
                    r_Y = Res()
                    with nc.sbuf_tensor(tg + "hin%d" % hf, [128, 3, L + 2], BF16) as hin, \
                            nc.sbuf_tensor(tg + "ct%d" % hf, [128, 3, L], F32) as ct, \
                            nc.sbuf_tensor(tg + "ubf%d" % hf, [128, L], BF16) as ubf:
                        r_hin, r_ct, r_ubf = Res(), Res(), Res()
                        S.op(pool, lambda: nc.gpsimd.memset(hin[:, :, 0:1], 0.0), [], [r_hin])
                        S.op(pool, lambda: nc.gpsimd.memset(hin[:, :, L + 1:L + 2], 0.0), [], [r_hin])
                        for cbk in range(4):
                            cb = hf * 4 + cbk
                            for jp in range(3):
                                row0 = (jp * 8 + cb) * 128
                                S.dma(sp, hin[:, jp, 1:L + 1], hyD[row0:row0 + 128, seq_t0:seq_t0 + L], [], [r_hin])
                            for jp in range(3):
                                q = jp * 8 + cb
                                w0 = vecs[:, l, V_CW + q:V_CW + q + 1]
                                w1 = vecs[:, l, V_CW + 24 + q:V_CW + 24 + q + 1]
                                w2 = vecs[:, l, V_CW + 48 + q:V_CW + 48 + q + 1]
                                bb = vecs[:, l, V_CB + q:V_CB + q + 1]
                                S.op(act, lambda: nc.scalar.activation(out=ct[:, jp, :], in_=hin[:, jp, 1:L + 1], func=AF.Identity,
                                                                       scale=w1, bias=bb), [r_hin, r_const], [r_ct])
                                S.op(dve, lambda: nc.vector.scalar_tensor_tensor(out=ct[:, jp, :], in0=hin[:, jp, 0:L], scalar=w0,
                                                                                 in1=ct[:, jp, :], op0=ALU.mult, op1=ALU.add),
                                     [r_hin, r_const, r_ct], [r_ct])
                                S.op(dve, lambda: nc.vector.scalar_tensor_tensor(out=ct[:, jp, :], in0=hin[:, jp, 2:L + 2], scalar=w2,
                                                                                 in1=ct[:, jp, :], op0=ALU.mult, op1=ALU.add),
                                     [r_hin, r_const, r_ct], [r_ct])
                            S.op(pool, lambda: nc.gpsimd.tensor_tensor(out=ct[:, 1, :], in0=ct[:, 1, :], in1=ct[:, 2, :], op=ALU.mult),
                                 [r_ct], [r_ct])
                            S.op(act, lambda: nc.scalar.copy(out=ubf[:], in_=ct[:, 1, :]), [r_ct], [r_ubf])
                            S.dma(pool, x0D[cb * 128:(cb + 1) * 128, 0:L], ct[:, 0, :], [r_ct], [DR("x0", cb)])
                            S.dma(pool, uD[cb * 128:(cb + 1) * 128, 0:L], ct[:, 1, :], [r_ct], [DR("u", cb)])
                            for t4 in range(0, nt, 4):
                                nb = min(4, nt - t4)
                                pt, pr = ps_next()
                                ptb = pt[:].bitcast(BF16)
                                for jj in range(nb):
                                    S.op(pe, lambda: nc.tensor.transpose(ptb[:, jj * 128:(jj + 1) * 128],
                                                                         ubf[:, (t4 + jj) * 128:(t4 + jj + 1) * 128], ident[:]),
                                         [r_ubf, r_const], [pr], inc=(jj == nb - 1))
                                copy_on(ev_eng(), uTM[:, t4:t4 + nb, cbk * 128:(cbk + 1) * 128],
                                        ptb[:, 0:nb * 128].rearrange("p (a b) -> p a b", b=128), [pr], [r_u])
                        S.barrier()
                    if DBG.get("cstage") == 1:
                        continue
                    with nc.sbuf_tensor(tg + "slb%d" % hf, [128, 2, nts, 2, 128], BF16) as slab, \
                            nc.sbuf_tensor(tg + "kf%d" % hf, [128, nD, 2, 512], BF16) as kf, \
                            nc.sbuf_tensor(tg + "us%d" % hf, [128, B, 2, 512], F32) as ust, \
                            nc.sbuf_tensor(tg + "pr%d" % hf, [128, 2, 4, 512], F32) as prd, \
                            nc.sbuf_tensor(tg + "ya%d" % hf, [128, 2, 2, 512], F32) as yacc:
                        slab_r = [Res(), Res()]
                        r_kf = Res()
                        yacc_r = [Res(), Res()]
                        ust_r = [Res() for _ in range(B)]
                        prd_r = [Res(), Res()]
                        pq = [0]
                        for j in range(nF):
                            si = j % 2
                            S.dma(sp, slab[:, si], tabf[0:Ls, j].rearrange("(i p) c f -> p i c f", p=128), [], [slab_r[si]])
                            for dd in range(nD):
                                r0 = (dd * nF + j) * 128
                                S.dma(act, kf[:, dd, 0, :], KreD[r0:r0 + 128, hf * 512:(hf + 1) * 512], [DR("Kf", dd, j, hf, 0)], [r_kf])
                                S.dma(act, kf[:, dd, 1, :], KimD[r0:r0 + 128, hf * 512:(hf + 1) * 512], [DR("Kf", dd, j, hf, 1)], [r_kf])
                            for i in range(B):
                                accs = [ps_next() for _ in range(2)]
                                for cs in range(2):
                                    for ic in range(nts):
                                        mm(accs[cs][0][:, :], slab[:, si, ic, cs, :], uTM[:, i * nts + ic, :], ic == 0, ic == nts - 1,
                                           [slab_r[si], r_u], [accs[cs][1]])
                                copy_on(act, ust[:, i, 0, :], accs[0][0][:, :], [accs[0][1]], [ust_r[i]])
                                copy_on(dve, ust[:, i, 1, :], accs[1][0][:, :], [accs[1][1]], [ust_r[i]])
                            for o in range(B):
                                E = dve if (o % 2 == 0) else pool
                                eb = o % 2
                                for i in range(B):
                                    dd = (o - i) + (B - 1)
                                    last_i = (i == B - 1)
                                    for q, (a_i, k_i) in enumerate(((0, 0), (1, 1), (0, 1), (1, 0))):
                                        S.op(E, lambda: E.e.tensor_tensor(out=prd[:, eb, q, :], in0=ust[:, i, a_i, :], in1=kf[:, dd, k_i, :],
                                                                          op=ALU.mult), [ust_r[i], r_kf], [prd_r[eb]])
                                    dre = Y[:, o, 0, j, :] if last_i else yacc[:, eb, 0, :]
                                    dim_ = Y[:, o, 1, j, :] if last_i else yacc[:, eb, 1, :]
                                    wr = [r_Y] if last_i else [yacc_r[eb]]
                                    if i == 0:
                                        S.op(E, lambda: E.e.tensor_tensor(out=dre, in0=prd[:, eb, 0, :], in1=prd[:, eb, 1, :], op=ALU.subtract),
                                             [prd_r[eb]], wr)
                                        S.op(E, lambda: E.e.tensor_tensor(out=dim_, in0=prd[:, eb, 2, :], in1=prd[:, eb, 3, :], op=ALU.add),
                                             [prd_r[eb]], wr)
                                    else:
                                        S.op(E, lambda: E.e.tensor_tensor(out=prd[:, eb, 0, :], in0=prd[:, eb, 0, :], in1=prd[:, eb, 1, :],
                                                                          op=ALU.subtract), [prd_r[eb]], [prd_r[eb]])
                                        S.op(E, lambda: E.e.tensor_tensor(out=prd[:, eb, 2, :], in0=prd[:, eb, 2, :], in1=prd[:, eb, 3, :],
                                                                          op=ALU.add), [prd_r[eb]], [prd_r[eb]])
                                        S.op(E, lambda: E.e.tensor_tensor(out=dre, in0=yacc[:, eb, 0, :], in1=prd[:, eb, 0, :], op=ALU.add),
                                             [prd_r[eb], yacc_r[eb]], wr)
                                        S.op(E, lambda: E.e.tensor_tensor(out=dim_, in0=yacc[:, eb, 1, :], in1=prd[:, eb, 2, :], op=ALU.add),
                                             [prd_r[eb], yacc_r[eb]], wr)
                        S.barrier()
                    with nc.sbuf_tensor(tg + "pc%d" % hf, [128, 5, 2, 512], BF16) as pcs, \
                            nc.sbuf_tensor(tg + "ex%d" % hf, [128, 4, 2, 512], F32) as ex, \
                            nc.sbuf_tensor(tg + "eg%d" % hf, [128, 4, 512], BF16) as eg, \
                            nc.sbuf_tensor(tg + "ey%d" % hf, [128, 4, 512], F32) as ey, \
                            nc.sbuf_tensor(tg + "es%d" % hf, [128, 2, 512], BF16) as es, \
                            nc.sbuf_tensor(tg + "eo%d" % hf, [128, 2, 512], BF16) as eo:
                        pcs_r = [Res() for _ in range(5)]
                        ex_r = [Res() for _ in range(4)]
                        eg_r = [Res() for _ in range(4)]
                        ey_r = [Res() for _ in range(4)]
                        es_r = [Res() for _ in range(2)]
                        eo_r = [Res() for _ in range(2)]
                        pcc = [0]
                        for o in range(B):
                            for tb in range(npc):
                                c0 = o * Ls + tb * pcw
                                for cbk in range(4):
                                    cb = hf * 4 + cbk
                                    S.dma(pool, ex[:, cbk, 0, 0:pcw], x0D[cb * 128:(cb + 1) * 128, c0:c0 + pcw], [DR("x0", cb)], [ex_r[cbk]])
                                    S.dma(pool, ex[:, cbk, 1, 0:pcw], uD[cb * 128:(cb + 1) * 128, c0:c0 + pcw], [DR("u", cb)], [ex_r[cbk]])
                                    S.dma(pool, eg[:, cbk, 0:pcw], ghD[cb * 128:(cb + 1) * 128, seq_t0 + c0:seq_t0 + c0 + pcw], [], [eg_r[cbk]])
                                accs = [ps_next() for _ in range(4)]
                                for j in range(nF):
                                    pi = pcc[0] % 5
                                    pcc[0] += 1
                                    S.dma(sp if j % 2 == 0 else act, pcs[:, pi, :, 0:pcw],
                                          tabt[j * 128:(j + 1) * 128, :, tb * pcw:(tb + 1) * pcw], [], [pcs_r[pi]])
                                    for cbk in range(4):
                                        mm(accs[cbk][0][:, 0:pcw], Y[:, o, 0, j, cbk * 128:(cbk + 1) * 128], pcs[:, pi, 0, 0:pcw], j == 0, False,
                                           [r_Y, pcs_r[pi]], [accs[cbk][1]])
                                        mm(accs[cbk][0][:, 0:pcw], Y[:, o, 1, j, cbk * 128:(cbk + 1) * 128], pcs[:, pi, 1, 0:pcw], False, j == nF - 1,
                                           [r_Y, pcs_r[pi]], [accs[cbk][1]], inc=(cbk == 3 or j == nF - 1))
                                for cbk in range(4):
                                    cb = hf * 4 + cbk
                                    ei = cbk
                                    S.op(dve, lambda: nc.vector.scalar_tensor_tensor(out=ey[:, ei, 0:pcw], in0=ex[:, ei, 1, 0:pcw],
                                                                                     scalar=vecs[:, l, V_HD + cb:V_HD + cb + 1],
                                                                                     in1=accs[cbk][0][:, 0:pcw], op0=ALU.mult, op1=ALU.add),
                                         [ex_r[ei], accs[cbk][1], r_const], [ey_r[ei]])
                                    S.op(pool, lambda: nc.gpsimd.tensor_tensor(out=ey[:, ei, 0:pcw], in0=ey[:, ei, 0:pcw], in1=ex[:, ei, 0, 0:pcw],
                                                                               op=ALU.mult), [ex_r[ei], ey_r[ei]], [ey_r[ei]])
                                    S.op(act, lambda: nc.scalar.activation(out=es[:, ei % 2, 0:pcw], in_=ey[:, ei, 0:pcw], func=AF.Square),
                                         [ey_r[ei]], [es_r[ei % 2]])
                                    S.op(pool, lambda: nc.gpsimd.tensor_tensor(out=eo[:, ei % 2, 0:pcw], in0=ey[:, ei, 0:pcw], in1=eg[:, ei, 0:pcw],
                                                                               op=ALU.mult), [ey_r[ei], eg_r[ei]], [eo_r[ei % 2]])
                                    S.dma(pool, yhD[cb * 128:(cb + 1) * 128, seq_t0 + c0:seq_t0 + c0 + pcw], eo[:, ei % 2, 0:pcw], [eo_r[ei % 2]],
                                          [DR("yh", cb, (seq_t0 + c0) // 128)])
                                    nq = pcw // 128
                                    pt, pr = ps_next()
                                    for q in range(nq):
                                        mm(pt[:, 8 * q:8 * q + 8], es[:, ei % 2, q * 128:(q + 1) * 128], ones_bf[:, 0:8], True, True,
                                           [es_r[ei % 2], r_const], [pr])
                                    tile0 = (seq_t0 + c0) // 128
                                    S.op(dve, lambda: nc.vector.tensor_tensor(out=sshAll[:, tile0:tile0 + nq], in0=pt[:, 0:8 * nq:8],
                                                                              in1=sshAll[:, tile0:tile0 + nq], op=ALU.add), [pr, r_ssh], [r_ssh])
                        S.barrier()

        if not last and not DBG.get("skipctx"):
            hyena_seq(0, LC, "c%d" % l, 1)
        hyena_seq(LC, LX, "x%d" % l, HB)
        with nc.sbuf_tensor("L%d_" % l + "rht", [128, 34], F32) as rht:
            r_rht = Res()
            S.op(act, lambda: nc.scalar.activation(out=rht[:], in_=sshAll[:], func=AF.Sqrt, scale=1.0 / 1024.0, bias=EPS), [r_ssh], [r_rht])
            S.op(dve, lambda: nc.vector.reciprocal(out=rhAll[:], in_=rht[:]), [r_rht], [r_ssh])
            S.barrier()
        if "C" in taps and l == 0:
            S.dma(sp, tap("yh", [1024, T], BF16)[:, :], yhD[:, :], [], [DR("tapyh")])
            S.dma(sp, tap("rh", [128, 34])[:, :], rhAll[:], [r_ssh], [DR("taprh")])
            if "Ck" in taps:
                S.dma(sp, tap("Kre", [4224, 1024])[:, :], KreD[:, :], [], [DR("tapkre")])
                S.dma(sp, tap("Kim", [4224, 1024])[:, :], KimD[:, :], [], [DR("tapkim")])
            if "Cu" in taps:
                S.dma(sp, tap("u", [1024, LX])[:, :], uD[:, :], [], [DR("tapu")])
                S.dma(sp, tap("x0", [1024, LX])[:, :], x0D[:, :], [], [DR("tapx0")])
            S.barrier()
        if stop == "C":
            break

        with nc.sbuf_tensor("L%d_" % l + "wout_s", [128, 16, D], BF16) as wout_s, \
                nc.sbuf_tensor("L%d_" % l + "ymt", [128, 2, 8, 128], BF16) as ymt, \
                nc.sbuf_tensor("L%d_" % l + "yht", [128, 2, 8, 128], BF16) as yht, \
                nc.sbuf_tensor("L%d_" % l + "xr", [128, 2, D], F32) as xr, \
                nc.sbuf_tensor("L%d_" % l + "zt", [128, 2, D], F32) as zt, \
                nc.sbuf_tensor("L%d_" % l + "zsq", [128, D], BF16) as zsq, \
                nc.sbuf_tensor("L%d_" % l + "dst", [128, 4], F32) as dst:
            r_wo, r_zsq, r_dst = Res(), Res(), Res()
            ym_r = [Res(), Res()]
            yh_r = [Res(), Res()]
            xr_r = [Res(), Res()]
            z_r = [Res(), Res()]
            S.dma(sp, wout_s[:], wout_b[:, :].rearrange("(k p) c -> p k c", p=128), [DR("wprep", l)], [r_wo])
            tiles = list(range(34)) if not last else list(range(2, 34))
            for ti, tl in enumerate(tiles):
                i = ti % 2
                isx = tl >= 2
                S.dma(sp, ymt[:, i], ymD[:, tl * 128:(tl + 1) * 128].rearrange("(k p) t -> p k t", p=128), [DR("ym", tl)], [ym_r[i]])
                S.dma(sp, yht[:, i], yhD[:, tl * 128:(tl + 1) * 128].rearrange("(k p) t -> p k t", p=128),
                      [DR("yh", cb, tl) for cb in range(8)], [yh_r[i]])
                rsrc = xsrc[(tl - 2) * 128:(tl - 1) * 128, :] if isx else csrc[tl * 128:(tl + 1) * 128, :]
                S.dma(sp, xr[:, i, :], rsrc, [DR("xres", tl)], [xr_r[i]])
                for cbk in range(4):
                    c0 = cbk * 512
                    pt, pr = ps_next()
                    for k in range(8):
                        mm(pt[:, :], ymt[:, i, k, :], wout_s[:, k, c0:c0 + 512], k == 0, k == 7, [ym_r[i], r_wo], [pr])
                    pt2, pr2 = ps_next()
                    for k in range(8):
                        mm(pt2[:, :], yht[:, i, k, :], wout_s[:, 8 + k, c0:c0 + 512], k == 0, k == 7, [yh_r[i], r_wo], [pr2])
                    S.op(act, lambda: nc.scalar.activation(out=zt[:, i, c0:c0 + 512], in_=pt[:, :], func=AF.Copy,
                                                           scale=rmAll[:, tl:tl + 1]), [pr, r_rm], [z_r[i]])
                    S.op(dve, lambda: nc.vector.scalar_tensor_tensor(out=zt[:, i, c0:c0 + 512], in0=pt2[:, :], scalar=rhAll[:, tl:tl + 1],
                                                                     in1=zt[:, i, c0:c0 + 512], op0=ALU.mult, op1=ALU.add),
                         [pr2, r_ssh, z_r[i]], [z_r[i]])
                S.op(pool, lambda: nc.gpsimd.memset(dst[:, 0:1], 0.0), [], [r_dst])
                S.op(act, lambda: nc.scalar.activation(out=zsq[:], in_=zt[:, i, :], func=AF.Square, accum_out=dst[:, 0:1]),
                     [z_r[i]], [r_zsq, r_dst])
                S.op(act, lambda: nc.scalar.activation(out=dst[:, 1:2], in_=dst[:, 0:1], func=AF.Sqrt, scale=1.0 / D, bias=EPS),
                     [r_dst], [r_dst])
                S.op(dve, lambda: nc.vector.reciprocal(out=dst[:, 2:3], in_=dst[:, 1:2]), [r_dst], [r_dst])
                gp = gpx if isx else gpc
                S.op(dve, lambda: nc.vector.scalar_tensor_tensor(out=zt[:, i, :], in0=zt[:, i, :], scalar=dst[:, 2:3], in1=gp[:],
                                                                 op0=ALU.mult, op1=ALU.mult), [z_r[i], r_dst, r_gp], [z_r[i]])
                S.op(pool, lambda: nc.gpsimd.tensor_tensor(out=zt[:, i, :], in0=zt[:, i, :], in1=xr[:, i, :], op=ALU.add),
                     [z_r[i], xr_r[i]], [z_r[i]])
                ddst = xdst[(tl - 2) * 128:(tl - 1) * 128, :] if isx else ctx1D[tl * 128:(tl + 1) * 128, :]
                S.dma(pool, ddst, zt[:, i, :], [z_r[i]], [DR("xres", tl)])
            S.barrier()
        if "D" in taps and l == 0:
            S.dma(sp, tap("x1", [LX, D])[:, :], x1D[:, :], [], [DR("tapx1")])
            S.dma(sp, tap("c1", [LC, D])[:, :], ctx1D[:, :], [], [DR("tapc1")])
            S.barrier()
        if stop == "D":
            break
    S.barrier()
    return nc, tap_out


def _pack_vecs(inp):
    v = np.zeros((128, DEPTH, 160), np.float32)
    for l in range(DEPTH):
        v[:, l, 0:16] = _cols(inp["pre_g"][l], 16)
        v[:, l, 16:20] = _cols(inp["q_norm_g"][l], 4)
        v[:, l, 20:22] = _cols(inp["kv_norm_g"][l], 2)
        v[:, l, 22:38] = _cols(np.concatenate([inp["grp_g_mla"][l], inp["grp_g_hy"][l]]), 16)
        for j in range(3):
            v[:, l, 38 + 24 * j:38 + 24 * (j + 1)] = _cols(inp["conv_w"][l, j], 24)
        v[:, l, 110:134] = _cols(inp["conv_b"][l], 24)
        v[:, l, 134:142] = _cols(inp["hy_D"][l], 8)
    return v


_CONST_CACHE = {}


def _host_consts():
    if not _CONST_CACHE:
        c = {}
        cosT, sinT, Rm = _rope_consts()
        Rm2 = np.zeros((128, 128), np.float32)
        Rm2[0:64, 0:64] = Rm
        Rm2[64:128, 64:128] = Rm
        c["ropeC"], c["ropeS"], c["Rm"] = np.concatenate([cosT, cosT], 0), np.concatenate([sinT, sinT], 0), Rm2
        deltas = np.linspace(math.log(1e-2) / 0.3, math.log(1e-2) / 1.5, 1024, dtype=np.float32)
        c["absd"] = np.abs(deltas).reshape(1, 1024).astype(np.float32)
        c["ident"] = np.eye(128, dtype=np.float32).astype(ml_dtypes.bfloat16)
        c["sgn"] = np.where(np.arange(128) % 2 == 0, 1.0, -1.0).astype(np.float32).reshape(128, 1)
        for L, Ls, s in ((LX, LX // HB, "x"), (LC, LC, "c")):
            c["tabF_" + s], c["tabT_" + s] = _dft_tables(Ls)
            c["zT_" + s], c["negt_" + s], _ = _filter_consts(L)
            c["wf_" + s] = _filter_consts(Ls)[2]
        _CONST_CACHE.update(c)
    return _CONST_CACHE


def make_in_maps(inp, ncores=NCORES):
    c = _host_consts()
    shared = dict(c)
    f32 = lambda a: np.ascontiguousarray(np.asarray(a, np.float32))
    for k in ("ada_w", "ada_b", "w_in", "w_uq", "w_ukv", "w_out", "post_g"):
        shared[k] = f32(inp[k])
    shared["vecs"] = _pack_vecs(inp)
    shared["filt_w1"] = f32(np.transpose(inp["filt_w1"], (1, 0, 2)))
    shared["filt_w2"] = f32(np.transpose(inp["filt_w2"], (1, 0, 2)))
    shared["filt_w3"] = f32(np.transpose(inp["filt_w3"], (1, 0, 2)))
    shared["filt_v"] = f32(np.stack([inp["filt_b1"].T, inp["filt_freq"].T, inp["filt_b2"].T], axis=-1))
    maps = []
    for b in range(ncores):
        m = dict(shared)
        m["x"] = f32(inp["x"][b])
        m["ctx"] = f32(inp["ctx"][b])
        cT = np.stack([_cols(inp["c"][b], 16), _cols(inp["c_ctx"], 16)], axis=-1)
        m["cT"] = f32(cT)
        maps.append(m)
    return maps


def kernel(**inputs):
    nc, _ = build_program()
    maps = make_in_maps(inputs)
    res = run_bass_kernel_spmd(nc, maps, core_ids=list(range(NCORES)))
    return np.stack([np.asarray(res.results[b]["out"], np.float32) for b in range(NCORES)], axis=0)
```

```python
import math
import numpy as np
import ml_dtypes
import concourse.bass as bass
import concourse.mybir as mybir
from concourse.bass_utils import run_bass_kernel_spmd

F32 = mybir.dt.float32
BF16 = mybir.dt.bfloat16
AF = mybir.ActivationFunctionType
ALU = mybir.AluOpType

D = 2048
NIN = 5952
LX = 4096
LC = 256
T = LC + LX
DEPTH = 2
H = 8
EPS = 1e-6
SCALE = 192.0 ** -0.5
OFF_Q, OFF_KV, OFF_KR, OFF_GM, OFF_HY, OFF_GH = 0, 512, 768, 832, 1856, 4928
MAGIC = 12582912.0
TWO_PI = 2.0 * math.pi
NCORES = 4
PIPE_DEPTH = 4
HB = 2
DBG = {}


class Res:
    __slots__ = ("w", "r")

    def __init__(self):
        self.w = None
        self.r = {}


class Eng:
    def __init__(self, name, e, si):
        self.name, self.e, self.si = name, e, si
        self.cnt = 0
        self.seen = {}


class Sched:
    NDS = 48

    def __init__(self, nc):
        self.nc = nc
        self.sems = []

        def mk(name, e):
            self.sems.append(nc.alloc_semaphore("sem_" + name))
            return Eng(name, e, len(self.sems) - 1)

        self.pe = mk("pe", nc.tensor)
        self.act = mk("act", nc.scalar)
        self.dve = mk("dve", nc.vector)
        self.pool = mk("pool", nc.gpsimd)
        self.sp = mk("sp", nc.sync)
        self.engs = [self.pe, self.act, self.dve, self.pool, self.sp]
        self.dq = []
        self.dq_sw = []
        for i in range(self.NDS):
            self.sems.append(nc.alloc_semaphore("sem_d%d" % i))
            (self.dq if i < 24 else self.dq_sw).append([len(self.sems) - 1, 0])
        self.dnext = {0: 0, 1: 0}
        self.rr = 0

    def _wait(self, X, toks):
        for si, val in toks:
            if X is self.pe and si == X.si:
                continue
            if X.seen.get(si, 0) < val:
                X.e.wait_ge(self.sems[si], val)
                X.seen[si] = val

    @staticmethod
    def _deps(reads, writes):
        toks = []
        for r in reads:
            if r.w is not None:
                toks.append(r.w)
        for w in writes:
            if w.w is not None:
                toks.append(w.w)
            toks.extend(w.r.items())
        return toks

    @staticmethod
    def _reg(tok, reads, writes):
        for r in reads:
            if r.r.get(tok[0], 0) < tok[1]:
                r.r[tok[0]] = tok[1]
        for w in writes:
            w.w = tok
            w.r = {}

    def op(self, X, fn, reads=(), writes=(), inc=True):
        self._wait(X, self._deps(reads, writes))
        ins = fn()
        if inc:
            X.cnt += 1
            ins.then_inc(self.sems[X.si], 1)
            tok = (X.si, X.cnt)
        else:
            tok = (X.si, X.cnt + 1)
        self._reg(tok, reads, writes)
        return ins

    def dma(self, Q, out, in_, reads=(), writes=(), **kw):
        sw = 1 if Q is self.pool else 0
        lst = self.dq_sw if sw else self.dq
        k = self.dnext[sw]
        self.dnext[sw] = (k + 1) % len(lst)
        d = lst[k]
        toks = self._deps(reads, writes)
        if d[1] > 0:
            toks.append((d[0], d[1]))
        self._wait(Q, toks)
        Q.e.dma_start(out=out, in_=in_, **kw).then_inc(self.sems[d[0]], 16)
        d[1] += 16
        self._reg((d[0], d[1]), reads, writes)

    def barrier(self):
        toks = [(E.si, E.cnt) for E in self.engs if E.cnt > 0]
        toks += [(d[0], d[1]) for d in self.dq + self.dq_sw if d[1] > 0]
        for X in self.engs:
            self._wait(X, toks)


def _freq_map(L):
    nt = L // 128
    ne, no = nt // 2 + 1, nt // 2
    fm = -np.ones((ne + no) * 128, np.int64)
    ev = np.arange(0, L + 1, 2)
    od = np.arange(1, L, 2)
    fm[:len(ev)] = ev
    fm[ne * 128:ne * 128 + len(od)] = od
    return fm, ne


def _dft_tables(L):
    N = 2 * L
    fm, _ = _freq_map(L)
    f = np.where(fm < 0, 0, fm)
    s_ = np.arange(L, dtype=np.int64)
    m = (s_[:, None] * f[None, :]) % N
    ang = m.astype(np.float64) * (2.0 * np.pi / N)
    C = np.cos(ang).astype(ml_dtypes.bfloat16)
    Sn = np.sin(ang).astype(ml_dtypes.bfloat16)
    nF = len(fm) // 128
    CS = np.stack([C, Sn], axis=1)
    tabF = np.ascontiguousarray(CS.reshape(L, 2, nF, 128).transpose(0, 2, 1, 3))
    tabT = np.ascontiguousarray(CS.transpose(2, 1, 0))
    return tabF, tabT


def _filter_consts(L):
    t = np.linspace(0.0, 1.0, L, dtype=np.float32)[:, None]
    bands = 16
    f = np.linspace(1e-4, bands - 1, bands, dtype=np.float32)[None, :]
    wpos = (np.float32(2.0 * math.pi) * np.arange(L, dtype=np.float32)[:, None] / np.float32(L)).astype(np.float32)
    z = np.concatenate([t, np.cos(f * wpos), -np.sin(f * wpos)], axis=-1).astype(np.float32)
    idx_rev = (L - np.arange(L)) % L
    zrev = z[idx_rev]
    tt = t[:, 0]
    trev = tt[idx_rev].copy()
    trev[0] = 1e4
    nt = L // 128
    negt = np.stack([(-tt).reshape(nt, 128).T, (-trev).reshape(nt, 128).T], axis=1)
    zT = np.stack([z.T, zrev.T], axis=0)
    N = 2 * L
    nF = L // 128 + 1
    fm, _ = _freq_map(L)
    wf = np.where(fm < 0, 0.0, np.where((fm == 0) | (fm == L), 1.0, 2.0)) / N
    wf = wf.reshape(nF, 128).T
    return (np.ascontiguousarray(zT, np.float32), np.ascontiguousarray(negt, np.float32),
            np.ascontiguousarray(wf, np.float32))


def _rope_consts():
    n = LX
    row = np.repeat(np.arange(n // 64, dtype=np.float32), 64)
    col = np.tile(np.arange(64, dtype=np.float32), n // 64)
    nf = 16
    inv = (np.float32(10000.0) ** (-np.arange(nf, dtype=np.float32) / np.float32(nf))).astype(np.float32)
    cosT = np.zeros((64, n), np.float32)
    sinT = np.zeros((64, n), np.float32)
    for d in range(64):
        pos = row if d < 32 else col
        ang = (pos * inv[d % 16]).astype(np.float32)
        cosT[d] = np.cos(ang)
        sinT[d] = np.sin(ang)
    Rm = np.zeros((64, 64), np.float32)
    for dp in range(64):
        if (dp % 32) < 16:
            Rm[dp + 16, dp] = -1.0
        else:
            Rm[dp - 16, dp] = 1.0
    return cosT, sinT, Rm


def _cols(v, nchunk):
    return np.ascontiguousarray(np.asarray(v, np.float32).reshape(nchunk, 128).T)


def build_program(stop=None, taps=()):
    nc = bass.Bass("TRN2", target_bir_lowering=False)
    S = Sched(nc)
    pe, act, dve, pool, sp = S.pe, S.act, S.dve, S.pool, S.sp
    taps = set(taps)

    def din(name, shape, dt=F32):
        return nc.dram_tensor(name, list(shape), dt, kind="ExternalInput").ap()

    def dscr(name, shape, dt):
        return nc.dram_tensor(name, list(shape), dt, kind="Internal").ap()

    x_in = din("x", [LX, D])
    ctx_in = din("ctx", [LC, D])
    cT_in = din("cT", [128, 16, 2])
    ada_w = din("ada_w", [DEPTH, D, 3 * D])
    ada_b = din("ada_b", [DEPTH, 3 * D])
    w_in = din("w_in", [DEPTH, D, NIN])
    w_uq = din("w_uq", [DEPTH, 512, 1536])
    w_ukv = din("w_ukv", [DEPTH, 256, 2048])
    w_out = din("w_out", [DEPTH, D, D])
    post_g = din("post_g", [DEPTH, D])
    vec_in = din("vecs", [128, DEPTH, 160])
    filt_w1 = din("filt_w1", [33, DEPTH, 64])
    filt_w2 = din("filt_w2", [64, DEPTH, 64])
    filt_w3 = din("filt_w3", [64, DEPTH, 2048])
    filt_v = din("filt_v", [64, DEPTH, 3])
    ropeC = din("ropeC", [128, LX])
    ropeS = din("ropeS", [128, LX])
    Rm_in = din("Rm", [128, 128])
    absd_in = din("absd", [1, 1024])
    ident_in = din("ident", [128, 128], BF16)
    sgn_in = din("sgn", [128, 1])
    LSX = LX // HB
    NFX = LSX // 128 + 1
    tabF = {LSX: din("tabF_x", [LSX, NFX, 2, 128], BF16), LC: din("tabF_c", [LC, 3, 2, 128], BF16)}
    tabT = {LSX: din("tabT_x", [NFX * 128, 2, LSX], BF16), LC: din("tabT_c", [384, 2, LC], BF16)}
    zT_in = {LX: din("zT_x", [2, 33, LX]), LC: din("zT_c", [2, 33, LC])}
    negt_in = {LX: din("negt_x", [128, 2, LX // 128]), LC: din("negt_c", [128, 2, LC // 128])}
    wf_in = {LSX: din("wf_x", [128, NFX]), LC: din("wf_c", [128, LC // 128 + 1])}
    out_x = nc.dram_tensor("out", [LX, D], F32, kind="ExternalOutput").ap()
    tap_out = {}

    def tap(name, shape, dt=F32):
        tap_out[name] = nc.dram_tensor("tap_" + name, list(shape), dt, kind="ExternalOutput").ap()
        return tap_out[name]

    win_b = dscr("win_b", [D, NIN], BF16)
    wout_b = dscr("wout_b", [D, D], BF16)
    wuq_b = dscr("wuq_b", [512, 1536], BF16)
    wukv_b = dscr("wukv_b", [256, 2048], BF16)
    modD = dscr("modD", [2, 3 * D], F32)
    hyD = dscr("hyD", [3072, T], BF16)
    gmD = dscr("gmD", [T, 1024], BF16)
    ghD = dscr("ghD", [1024, T], BF16)
    oD = dscr("oD", [T, 1024], F32)
    ymD = dscr("ymD", [1024, T], BF16)
    yhD = dscr("yhD", [1024, T], BF16)
    x0D = dscr("x0D", [1024, LX], F32)
    uD = dscr("uD", [1024, LX], F32)
    KreD = dscr("KreD", [8192, 1024], BF16)
    KimD = dscr("KimD", [8192, 1024], BF16)
    x1D = dscr("x1D", [LX, D], F32)
    ctx1D = dscr("ctx1D", [LC, D], F32)

    dres = {}

    def DR(*key):
        if key not in dres:
            dres[key] = Res()
        return dres[key]

    def sb(name, shape, dt):
        return nc.alloc_sbuf_tensor("g_" + name, list(shape), dt)

    ident = sb("ident", [128, 128], BF16)
    ones_bf = sb("ones_bf", [128, 128], BF16)
    vecs = sb("vecs", [128, DEPTH, 160], F32)
    Rm_sb = sb("Rm_sb", [128, 128], F32)
    sgn = sb("sgn", [128, 1], F32)
    rmAll = sb("rmAll", [128, 34], F32)
    sshAll = sb("sshAll", [128, 34], F32)
    rhAll = sb("rhAll", [128, 34], F32)
    modc = sb("modc", [128, 4, 16], F32)
    axc = sb("axc", [128, 2, 16], F32)
    gpx = sb("gpx", [128, D], F32)
    gpc = sb("gpc", [128, D], F32)
    r_const = Res()
    r_mod = Res()
    r_gp = Res()
    r_rm = Res()
    r_ssh = Res()

    psb = [nc.alloc_psum_tensor("psb%d" % i, [128, 512], F32) for i in range(8)]
    psr = [Res() for _ in range(8)]
    psn = [0]
    ps_set = [list(range(8))]

    def ps_next():
        psn[0] = (psn[0] + 1) % len(ps_set[0])
        i = ps_set[0][psn[0]]
        return psb[i], psr[i]

    rr = [0]

    def ev_eng():
        rr[0] ^= 1
        return act if rr[0] else dve

    def mm(out, lhsT, rhs, start, stop, reads, writes, inc=None):
        S.op(pe, lambda: nc.tensor.matmul(out, lhsT=lhsT, rhs=rhs, start=start, stop=stop),
             reads, writes, inc=(stop if inc is None else inc))

    def copy_on(E, out, in_, reads, writes, scale=None):
        if E is act:
            if scale is None:
                S.op(act, lambda: nc.scalar.copy(out=out, in_=in_), reads, writes)
            else:
                S.op(act, lambda: nc.scalar.activation(out=out, in_=in_, func=AF.Copy, scale=scale), reads, writes)
        else:
            if scale is None:
                S.op(E, lambda: E.e.tensor_copy(out=out, in_=in_), reads, writes)
            else:
                S.op(E, lambda: E.e.tensor_scalar(out=out, in0=in_, scalar1=scale, scalar2=None, op0=ALU.mult),
                     reads, writes)

    def rsqrt_col(dst, src, n, reads, writes, tmp):
        S.op(act, lambda: nc.scalar.activation(out=tmp, in_=src, func=AF.Sqrt, scale=1.0 / n, bias=EPS),
             reads, writes)
        S.op(dve, lambda: nc.vector.reciprocal(out=dst, in_=tmp), writes, writes)

    S.dma(sp, ident[:], ident_in[:, :], [], [r_const])
    S.dma(sp, vecs[:], vec_in[:, :, :], [], [r_const])
    S.dma(sp, Rm_sb[:], Rm_in[:, :], [], [r_const])
    S.dma(sp, sgn[:], sgn_in[:, :], [], [r_const])
    S.op(pool, lambda: nc.gpsimd.memset(ones_bf[:], 1.0), [], [r_const])

    V_PRE, V_QG, V_KVG, V_GCAT, V_CW, V_CB, V_HD = 0, 16, 20, 22, 38, 110, 134

    for l in range(DEPTH):
        last = (l == DEPTH - 1)
        xsrc = x_in if l == 0 else x1D
        csrc = ctx_in if l == 0 else ctx1D
        xdst = out_x if last else x1D

        with nc.sbuf_tensor("L%d_" % l + "wst", [128, 3, 2048], F32) as wst, nc.sbuf_tensor("L%d_" % l + "wcv", [128, 3, 2048], BF16) as wcv:
            wst_r = [Res() for _ in range(3)]
            wcv_r = [Res() for _ in range(3)]
            cnt = [0]

            def prep(src, dst, rows, c0, c1, scale, dkey):
                i = cnt[0] % 3
                cnt[0] += 1
                n = c1 - c0
                S.dma(sp, wst[:, i, 0:n], src[rows * 128:(rows + 1) * 128, c0:c1], [], [wst_r[i]])
                E = [act, dve, pool][cnt[0] % 3]
                copy_on(E, wcv[:, i, 0:n], wst[:, i, 0:n], [wst_r[i], r_const], [wcv_r[i]], scale)
                S.dma(pool, dst[rows * 128:(rows + 1) * 128, c0:c1], wcv[:, i, 0:n], [wcv_r[i]], [DR(dkey, rows, c0)])

            for k in range(16):
                for j in range(3):
                    prep(w_in[l], win_b, k, j * 1984, (j + 1) * 1984, None, "win")
            for k in range(16):
                prep(w_out[l], wout_b, k, 0, 2048, vecs[:, l, V_GCAT + k:V_GCAT + k + 1], "wout")
            for k in range(4):
                prep(w_uq[l], wuq_b, k, 0, 1536, vecs[:, l, V_QG + k:V_QG + k + 1], "wuq")
            for k in range(2):
                prep(w_ukv[l], wukv_b, k, 0, 2048, vecs[:, l, V_KVG + k:V_KVG + k + 1], "wukv")
            S.barrier()

        with nc.sbuf_tensor("L%d_" % l + "siluc", [128, 16, 2], F32) as siluc, \
                nc.sbuf_tensor("L%d_" % l + "adw", [128, 2, 16, 256], F32) as adw, \
                nc.sbuf_tensor("L%d_" % l + "adb", [2, 3 * D], F32) as adb, \
                nc.sbuf_tensor("L%d_" % l + "modsb", [2, 3 * D], F32) as modsb, \
                nc.sbuf_tensor("L%d_" % l + "pgb", [128, D], F32) as pgb:
            r_s, r_adb, r_ms, r_pg = Res(), Res(), Res(), Res()
            adw_r = [Res(), Res()]
            S.dma(sp, siluc[:], cT_in[:, :, :], [], [r_s])
            S.op(act, lambda: nc.scalar.activation(out=siluc[:], in_=siluc[:], func=AF.Silu), [r_s], [r_s])
            S.dma(sp, adb[:], ada_b[l:l + 1, :].partition_broadcast(2), [], [r_adb])
            for cb in range(24):
                i = cb % 2
                S.dma(sp, adw[:, i], ada_w[l, :, cb * 256:(cb + 1) * 256].rearrange("(k p) c -> p k c", p=128),
                      [], [adw_r[i]])
                pt, pr = ps_next()
                for k in range(16):
                    mm(pt[0:2, 0:256], siluc[:, k, :], adw[:, i, k, :], k == 0, k == 15, [r_s, adw_r[i]], [pr])
                S.op(dve, lambda: nc.vector.tensor_tensor(out=modsb[:, cb * 256:(cb + 1) * 256], in0=pt[0:2, 0:256],
                                                          in1=adb[:, cb * 256:(cb + 1) * 256], op=ALU.add),
                     [pr, r_adb], [r_ms])
            S.dma(sp, modD[:, :], modsb[:], [r_ms], [DR("mod")])
            for r in range(2):
                for j in range(2):
                    S.dma(sp, modc[:, 2 * r + j, :], modD[r, j * D:(j + 1) * D].rearrange("(k p) -> p k", p=128),
                          [DR("mod")], [r_mod], allow_slow_non_contiguous=True)
            S.dma(sp, gpx[:], modD[0:1, 2 * D:3 * D].partition_broadcast(128), [DR("mod")], [r_gp])
            S.dma(sp, gpc[:], modD[1:2, 2 * D:3 * D].partition_broadcast(128), [DR("mod")], [r_gp])
            S.dma(sp, pgb[:], post_g[l:l + 1, :].partition_broadcast(128), [], [r_pg])
            for r in range(2):
                S.op(dve, lambda: nc.vector.scalar_tensor_tensor(
                    out=axc[:, r, :], in0=modc[:, 2 * r + 1, :], scalar=1.0, in1=vecs[:, l, V_PRE:V_PRE + 16],
                    op0=ALU.add, op1=ALU.mult), [r_mod, r_const], [r_mod])
            S.op(dve, lambda: nc.vector.tensor_tensor(out=gpx[:], in0=gpx[:], in1=pgb[:], op=ALU.mult), [r_gp, r_pg], [r_gp])
            S.op(pool, lambda: nc.gpsimd.tensor_tensor(out=gpc[:], in0=gpc[:], in1=pgb[:], op=ALU.mult), [r_gp, r_pg], [r_gp])
            S.barrier()
        if "mod" in taps and l == 0:
            S.dma(sp, tap("mod", [2, 3 * D])[:, :], modD[:, :], [DR("mod")], [DR("tapmod")])
        if stop == "mod":
            break

        with nc.sbuf_tensor("L%d_" % l + "pqT", [128, 4, T], BF16) as pqT, \
                nc.sbuf_tensor("L%d_" % l + "ckvT", [128, 2, T], BF16) as ckvT, \
                nc.sbuf_tensor("L%d_" % l + "krT", [128, T], BF16) as krT:
            groups = [(0, LC)] + [(LC + 512 * g, 512) for g in range(8)]
            r_pq = [Res() for _ in groups]
            r_ckv = [Res() for _ in groups]
            r_kr = [Res() for _ in groups]

            with nc.sbuf_tensor("L%d_" % l + "xt", [128, 2, D], F32) as xt, \
                    nc.sbuf_tensor("L%d_" % l + "xn", [128, 4, D], BF16) as xn, \
                    nc.sbuf_tensor("L%d_" % l + "stat", [128, 8], F32) as stat, \
                    nc.sbuf_tensor("L%d_" % l + "hxT", [128, 2, 16, 512], BF16) as hxT, \
                    nc.sbuf_tensor("L%d_" % l + "wblk", [128, 2, 16, 512], BF16) as wblk, \
                    nc.sbuf_tensor("L%d_" % l + "ost", [128, 4, 512], BF16) as ost, \
                    nc.sbuf_tensor("L%d_" % l + "qraw", [128, 4, 512], F32) as qraw, \
                    nc.sbuf_tensor("L%d_" % l + "qsq", [128, 4, 512], BF16) as qsq, \
                    nc.sbuf_tensor("L%d_" % l + "rbc", [128, 512], F32) as rbc, \
                    nc.sbuf_tensor("L%d_" % l + "krw", [128, 4, 512], F32) as krw, \
                    nc.sbuf_tensor("L%d_" % l + "wkr2", [128, 16, 128], BF16) as wkr2:
                xt_r = [Res(), Res()]
                xn_r = [Res() for _ in range(4)]
                r_sqj, r_stat = Res(), Res()
                r_wkr = Res()
                for dpl in range(2):
                    S.dma(sp, wkr2[:, :, dpl * 64:(dpl + 1) * 64], win_b[:, OFF_KR:OFF_KR + 64].rearrange("(k p) c -> p k c", p=128),
                          [DR("win", k, 0) for k in range(16)], [r_wkr])
                hx_r = [Res(), Res()]
                wb_r = [Res(), Res()]
                ost_r = [Res() for _ in range(4)]
                r_qraw, r_qsq, r_rbc, r_krw = Res(), Res(), Res(), Res()
                wbc = [0]
                tcount = [0]
                wblocks = [(0, 512, "q"), (512, 320, "kv"), (832, 512, "gm"), (1344, 512, "gm")]
                wblocks += [(OFF_HY + 512 * j, 512, "hy") for j in range(6)]
                wblocks += [(OFF_GH + 512 * j, 512, "gh") for j in range(2)]
                def hx_part1(gi):
                    t0, n = groups[gi]
                    isx = t0 >= LC
                    for tt in range(n // 128):
                        i = tcount[0] % 2
                        tcount[0] += 1
                        src = (xsrc[t0 - LC + tt * 128:t0 - LC + (tt + 1) * 128, :] if isx
                               else csrc[t0 + tt * 128:t0 + (tt + 1) * 128, :])
                        rsrc = DR("xres", (t0 + tt * 128) // 128)
                        S.dma(sp, xt[:, i, :], src, [rsrc], [xt_r[i]])
                        S.op(pool, lambda: nc.gpsimd.memset(stat[:, 0:1], 0.0), [], [r_stat])
                        S.op(act, lambda: nc.scalar.activation(out=xn[:, tt, :], in_=xt[:, i, :], func=AF.Square,
                                                               accum_out=stat[:, 0:1]), [xt_r[i]], [xn_r[tt], r_stat])
                        rsqrt_col(stat[:, 2:3], stat[:, 0:1], D, [r_stat], [r_stat], stat[:, 1:2])
                        S.op(act, lambda: nc.scalar.activation(out=xn[:, tt, :], in_=xt[:, i, :], func=AF.Copy,
                                                               scale=stat[:, 2:3]), [xt_r[i], r_stat], [xn_r[tt]])

                def hx_part2(gi):
                    t0, n = groups[gi]
                    isx = t0 >= LC
                    mi = 0 if isx else 1
                    hb = gi % 2
                    for tt in range(n // 128):
                        for kq in range(4):
                            pt, pr = ps_next()
                            ptb = pt[:].bitcast(BF16)
                            for kk in range(4):
                                k = kq * 4 + kk
                                S.op(pe, lambda: nc.tensor.transpose(ptb[:, kk * 128:(kk + 1) * 128],
                                                                     xn[:, tt, k * 128:(k + 1) * 128], ident[:]),
                                     [xn_r[tt], r_const], [pr], inc=(kk == 3))
                            for kk in range(4):
                                k = kq * 4 + kk
                                E = ev_eng()
                                o_ap = hxT[:, hb, k, tt * 128:(tt + 1) * 128]
                                i_ap = ptb[:, kk * 128:(kk + 1) * 128]
                                a_ap = axc[:, mi, k:k + 1]
                                s_ap = modc[:, 2 * mi, k:k + 1]
                                if E is act:
                                    S.op(act, lambda: nc.scalar.activation(out=o_ap, in_=i_ap, func=AF.Identity,
                                                                           scale=a_ap, bias=s_ap),
                                         [pr, r_mod], [hx_r[hb]])
                                else:
                                    S.op(dve, lambda: nc.vector.tensor_scalar(out=o_ap, in0=i_ap, scalar1=a_ap,
                                                                              scalar2=s_ap, op0=ALU.mult, op1=ALU.add),
                                         [pr, r_mod], [hx_r[hb]])

                hx_part1(0)
                hx_part2(0)
                for gi, (t0, n) in enumerate(groups):
                    isx = t0 >= LC
                    hb = gi % 2
                    for bi, (c0, ncol, kind) in enumerate(wblocks):
                        if gi + 1 < len(groups):
                            if bi == 2:
                                hx_part1(gi + 1)
                            if bi == 8:
                                hx_part2(gi + 1)
                        wi = wbc[0] % 2
                        wbc[0] += 1
                        S.dma(sp, wblk[:, wi, :, 0:ncol],
                              win_b[:, c0:c0 + ncol].rearrange("(k p) c -> p k c", p=128),
                              [DR("win", k, (c0 // 1984) * 1984) for k in range(16)] +
                              [DR("win", k, ((c0 + ncol - 1) // 1984) * 1984) for k in range(16)], [wb_r[wi]])
                        if kind in ("q", "kv"):
                            nblk = 4 if kind == "q" else 2
                            dim = 512.0 if kind == "q" else 256.0
                            pss = []
                            for m in range(nblk):
                                pt, pr = ps_next()
                                for k in range(16):
                                    mm(pt[:, 0:n], wblk[:, wi, k, m * 128:(m + 1) * 128], hxT[:, hb, k, 0:n],
                                       k == 0, k == 15, [wb_r[wi], hx_r[hb]], [pr])
                                pss.append((pt, pr))
                            for m, (pt, pr) in enumerate(pss):
                                S.op(act, lambda: nc.scalar.copy(out=qraw[:, m, 0:n], in_=pt[:, 0:n]), [pr], [r_qraw])
                                S.op(act, lambda: nc.scalar.activation(out=qsq[:, m, 0:n], in_=pt[:, 0:n], func=AF.Square),
                                     [pr], [r_qsq])
                            pt, pr = ps_next()
                            for m in range(nblk):
                                mm(pt[:, 0:n], ones_bf[:], qsq[:, m, 0:n], m == 0, m == nblk - 1, [r_qsq, r_const], [pr])
                            S.op(act, lambda: nc.scalar.activation(out=rbc[:, 0:n], in_=pt[:, 0:n], func=AF.Sqrt,
                                                                   scale=1.0 / dim, bias=EPS), [pr], [r_rbc])
                            S.op(dve, lambda: nc.vector.reciprocal(out=rbc[:, 0:n], in_=rbc[:, 0:n]), [r_rbc], [r_rbc])
                            dstT, dres_ = (pqT, r_pq[gi]) if kind == "q" else (ckvT, r_ckv[gi])
                            for m in range(nblk):
                                E = dve if m % 2 == 0 else pool
                                S.op(E, lambda: E.e.tensor_tensor(out=dstT[:, m, t0:t0 + n], in0=qraw[:, m, 0:n],
                                                                  in1=rbc[:, 0:n], op=ALU.mult),
                                     [r_qraw, r_rbc], [dres_])
                            if kind == "kv":
                                pt, pr = ps_next()
                                for k in range(16):
                                    mm(pt[:, 0:n], wkr2[:, k, :], hxT[:, hb, k, 0:n], k == 0, k == 15,
                                       [r_wkr, hx_r[hb]], [pr])
                                if not isx:
                                    S.op(act, lambda: nc.scalar.copy(out=krT[:, t0:t0 + n], in_=pt[:, 0:n]), [pr], [r_kr[gi]])
                                else:
                                    p0 = t0 - LC
                                    S.op(act, lambda: nc.scalar.copy(out=krw[:, 0, 0:n], in_=pt[:, 0:n]), [pr], [r_krw])
                                    S.dma(sp, krw[:, 1, 0:n], ropeC[:, p0:p0 + n], [], [r_krw])
                                    S.dma(sp, krw[:, 2, 0:n], ropeS[:, p0:p0 + n], [], [r_krw])
                                    pt2, pr2 = ps_next()
                                    mm(pt2[:, 0:n], Rm_sb[:], krw[:, 0, 0:n], True, True, [r_krw, r_const], [pr2])
                                    S.op(dve, lambda: nc.vector.tensor_tensor(out=krw[:, 3, 0:n], in0=pt2[:, 0:n],
                                                                              in1=krw[:, 2, 0:n], op=ALU.mult), [pr2, r_krw], [r_krw])
                                    S.op(pool, lambda: nc.gpsimd.tensor_tensor(out=krw[:, 1, 0:n], in0=krw[:, 0, 0:n],
                                                                               in1=krw[:, 1, 0:n], op=ALU.mult), [r_krw], [r_krw])
                                    S.op(dve, lambda: nc.vector.tensor_tensor(out=krT[:, t0:t0 + n], in0=krw[:, 1, 0:n],
                                                                              in1=krw[:, 3, 0:n], op=ALU.add), [r_krw], [r_kr[gi]])
                        elif kind == "gm":
                            cg = (c0 - OFF_GM) // 512
                            for tt in range(n // 128):
                                pt, pr = ps_next()
                                for k in range(16):
                                    mm(pt[:, :], hxT[:, hb, k, tt * 128:(tt + 1) * 128], wblk[:, wi, k, :], k == 0, k == 15,
                                       [wb_r[wi], hx_r[hb]], [pr])
                                oi = tt % 4
                                S.op(act, lambda: nc.scalar.activation(out=ost[:, oi, :], in_=pt[:, :], func=AF.Silu),
                                     [pr], [ost_r[oi]])
                                S.dma(pool, gmD[t0 + tt * 128:t0 + (tt + 1) * 128, cg * 512:(cg + 1) * 512], ost[:, oi, :],
                                      [ost_r[oi]], [DR("gm", (t0 + tt * 128) // 128)])
                        else:
                            for m in range(4):
                                pt, pr = ps_next()
                                for k in range(16):
                                    mm(pt[:, 0:n], wblk[:, wi, k, m * 128:(m + 1) * 128], hxT[:, hb, k, 0:n], k == 0, k == 15,
                                       [wb_r[wi], hx_r[hb]], [pr])
                                oi = m
                                if kind == "hy":
                                    E = ev_eng()
                                    copy_on(E, ost[:, oi, 0:n], pt[:, 0:n], [pr], [ost_r[oi]])
                                    row0 = (c0 - OFF_HY) + m * 128
                                    S.dma(pool, hyD[row0:row0 + 128, t0:t0 + n], ost[:, oi, 0:n], [ost_r[oi]],
                                          [DR("hy", row0 // 128, gi)])
                                else:
                                    S.op(act, lambda: nc.scalar.activation(out=ost[:, oi, 0:n], in_=pt[:, 0:n], func=AF.Silu),
                                         [pr], [ost_r[oi]])
                                    row0 = (c0 - OFF_GH) + m * 128
                                    S.dma(pool, ghD[row0:row0 + 128, t0:t0 + n], ost[:, oi, 0:n], [ost_r[oi]],
                                          [DR("gh", row0 // 128, gi)])
                S.barrier()
            if "A" in taps and l == 0:
                with nc.sbuf_tensor("L%d_" % l + "tapst", [128, T], F32) as tapst:
                    r_t = Res()
                    tq = tap("pq", [4, 128, T])
                    tk = tap("ckv", [2, 128, T])
                    tr = tap("kr", [64, T])
                    for m in range(4):
                        S.op(dve, lambda: nc.vector.tensor_copy(out=tapst[:], in_=pqT[:, m, :]), r_pq, [r_t])
                        S.dma(sp, tq[m], tapst[:], [r_t], [DR("tapq", m)])
                    for m in range(2):
                        S.op(dve, lambda: nc.vector.tensor_copy(out=tapst[:], in_=ckvT[:, m, :]), r_ckv, [r_t])
                        S.dma(sp, tk[m], tapst[:], [r_t], [DR("tapk", m)])
                    S.op(dve, lambda: nc.vector.tensor_copy(out=tapst[0:64, :], in_=krT[0:64, :]), r_kr, [r_t])
                    S.dma(sp, tr[:, :], tapst[0:64, :], [r_t], [DR("tapr")])
                    S.dma(sp, tap("hy", [3072, T], BF16)[:, :], hyD[:, :], [DR("hy", a, b) for a in range(24) for b in range(9)], [DR("taphy")])
                    S.dma(sp, tap("gm", [T, 1024], BF16)[:, :], gmD[:, :], [DR("gm", a) for a in range(34)], [DR("tapgm")])
                    S.dma(sp, tap("gh", [1024, T], BF16)[:, :], ghD[:, :], [DR("gh", a, b) for a in range(8) for b in range(9)], [DR("tapgh")])
                    S.barrier()
            if stop == "A":
                break

            do_ctx_q = not last
            with nc.sbuf_tensor("L%d_" % l + "wuq", [128, 4, 1536], BF16) as wuq_s, \
                    nc.sbuf_tensor("L%d_" % l + "wukv", [128, 2, 2048], BF16) as wukv_s, \
                    nc.sbuf_tensor("L%d_" % l + "KhT", [128, 2, T], BF16) as KhT, \
                    nc.sbuf_tensor("L%d_" % l + "Vh", [128, 2, 34, 132], BF16) as Vh, \
                    nc.sbuf_tensor("L%d_" % l + "QnT", [128, 2, 512], BF16) as QnT, \
                    nc.sbuf_tensor("L%d_" % l + "QrT", [128, 2, 512], BF16) as QrT, \
                    nc.sbuf_tensor("L%d_" % l + "qrw", [128, 4, 512], F32) as qrw, \
                    nc.sbuf_tensor("L%d_" % l + "wqr2", [128, H, 4, 128], BF16) as wqr2, \
                    nc.sbuf_tensor("L%d_" % l + "PT", [128, 8, 512], BF16) as PT, \
                    nc.sbuf_tensor("L%d_" % l + "osb", [128, 8, 128], F32) as osb, \
                    nc.sbuf_tensor("L%d_" % l + "rec", [128, 8], F32) as rec:
                r_w, r_qrw, r_rec = Res(), Res(), Res()
                r_K = [Res(), Res()]
                r_V = [Res(), Res()]
                qn_r = [Res(), Res()]
                qr_r = [Res(), Res()]
                pt_r = [Res() for _ in range(8)]
                osb_r = [Res() for _ in range(8)]
                S.dma(sp, wuq_s[:], wuq_b[:, :].rearrange("(k p) c -> p k c", p=128), [DR("wuq", k, 0) for k in range(4)], [r_w])
                S.dma(sp, wukv_s[:], wukv_b[:, :].rearrange("(k p) c -> p k c", p=128), [DR("wukv", k, 0) for k in range(2)], [r_w])
                for kb_ in range(2):
                    S.op(pool, lambda: nc.gpsimd.memset(Vh[:, kb_, :, 128:129], 1.0), [], [r_V[kb_]])
                for hh in range(H):
                    for dpl in range(2):
                        S.dma(sp, wqr2[:, hh, :, dpl * 64:(dpl + 1) * 64],
                              wuq_b[:, hh * 192 + 128:hh * 192 + 192].rearrange("(k p) c -> p k c", p=128),
                              [DR("wuq", k, 0) for k in range(4)], [r_w])
                ps_set[0] = [0, 1, 2, 3]
                accs2 = [[(psb[4], psr[4], 0), (psb[4], psr[4], 256), (psb[5], psr[5], 0), (psb[5], psr[5], 256)],
                         [(psb[6], psr[6], 0), (psb[6], psr[6], 256), (psb[7], psr[7], 0), (psb[7], psr[7], 256)]]
                qcount = [0]
                pcount = [0]

                def grp_of_tile(kt):
                    return 0 if kt < 2 else 1 + (kt * 128 - LC) // 512

                def build_kv(h, kb):
                    for gi, (t0, n) in enumerate(groups):
                        pt, pr = ps_next()
                        for kc in range(2):
                            mm(pt[:, 0:n], wukv_s[:, kc, h * 256:h * 256 + 128], ckvT[:, kc, t0:t0 + n], kc == 0, kc == 1,
                               [r_w, r_ckv[gi]], [pr])
                        copy_on(dve, KhT[:, kb, t0:t0 + n], pt[:, 0:n], [pr], [r_K[kb]])
                    for kt0 in range(0, 34, 4):
                        nb = min(4, 34 - kt0)
                        pt, pr = ps_next()
                        for j in range(nb):
                            kt = kt0 + j
                            for kc in range(2):
                                mm(pt[:, j * 128:(j + 1) * 128], ckvT[:, kc, kt * 128:(kt + 1) * 128],
                                   wukv_s[:, kc, h * 256 + 128:h * 256 + 256], kc == 0, kc == 1,
                                   [r_w, r_ckv[grp_of_tile(kt)]], [pr])
                        copy_on(dve, Vh[:, kb, kt0:kt0 + nb, 0:128],
                                pt[:, 0:nb * 128].rearrange("p (a b) -> p a b", b=128), [pr], [r_V[kb]])

                def build_q(h, qi, qb):
                    t0, n = groups[qi]
                    isx = t0 >= LC
                    pt, pr = ps_next()
                    for kc in range(4):
                        mm(pt[:, 0:n], wuq_s[:, kc, h * 192:h * 192 + 128], pqT[:, kc, t0:t0 + n], kc == 0, kc == 3,
                           [r_w, r_pq[qi]], [pr])
                    copy_on(dve, QnT[:, qb, 0:n], pt[:, 0:n], [pr], [qn_r[qb]], SCALE)
                    pt2, pr2 = ps_next()
                    for kc in range(4):
                        mm(pt2[:, 0:n], wqr2[:, h, kc, :], pqT[:, kc, t0:t0 + n], kc == 0, kc == 3,
                           [r_w, r_pq[qi]], [pr2])
                    if not isx:
                        copy_on(dve, QrT[:, qb, 0:n], pt2[:, 0:n], [pr2], [qr_r[qb]], SCALE)
                    else:
                        p0 = t0 - LC
                        copy_on(dve, qrw[:, 0, 0:n], pt2[:, 0:n], [pr2], [r_qrw], SCALE)
                        S.dma(sp, qrw[:, 1, 0:n], ropeC[:, p0:p0 + n], [], [r_qrw])
                        S.dma(sp, qrw[:, 2, 0:n], ropeS[:, p0:p0 + n], [], [r_qrw])
                        pt3, pr3 = ps_next()
                        mm(pt3[:, 0:n], Rm_sb[:], qrw[:, 0, 0:n], True, True, [r_qrw, r_const], [pr3])
                        S.op(dve, lambda: nc.vector.tensor_tensor(out=qrw[:, 3, 0:n], in0=pt3[:, 0:n],
                                                                  in1=qrw[:, 2, 0:n], op=ALU.mult), [pr3, r_qrw], [r_qrw])
                        S.op(pool, lambda: nc.gpsimd.tensor_tensor(out=qrw[:, 1, 0:n], in0=qrw[:, 0, 0:n],
                                                                   in1=qrw[:, 1, 0:n], op=ALU.mult), [r_qrw], [r_qrw])
                        S.op(dve, lambda: nc.vector.tensor_tensor(out=QrT[:, qb, 0:n], in0=qrw[:, 1, 0:n],
                                                                  in1=qrw[:, 3, 0:n], op=ALU.add), [r_qrw], [qr_r[qb]])

                items = [(h, qi) for h in range(H) for qi, (t0, n) in enumerate(groups) if (t0 >= LC or do_ctx_q)]
                build_kv(0, 0)
                build_q(items[0][0], items[0][1], 0)
                for idx, (h, qi) in enumerate(items):
                    t0, n = groups[qi]
                    isx = t0 >= LC
                    kts = list(range(34)) if isx else [0, 1]
                    qb = idx % 2
                    kb = h % 2
                    acc = accs2[qb]
                    if idx + 1 < len(items):
                        nh, nqi = items[idx + 1]
                        if nh != h:
                            build_kv(nh, nh % 2)
                        build_q(nh, nqi, (idx + 1) % 2)
                    nq = n // 128

                    def pv(ki, kt, pb):
                        for j in range(nq):
                            at, ar, c0 = acc[j]
                            mm(at[:, c0:c0 + 129], PT[:, pb, j * 128:(j + 1) * 128], Vh[:, kb, kt, 0:129],
                               ki == 0 and c0 == 0, ki == len(kts) - 1, [pt_r[pb], r_V[kb]], [ar])

                    pend = []
                    for ki0 in range(0, len(kts), 2):
                        pair = []
                        for dk in range(2):
                            kt = kts[ki0 + dk]
                            pt, pr = ps_next()
                            mm(pt[:, 0:n], KhT[:, kb, kt * 128:(kt + 1) * 128], QnT[:, qb, 0:n], True, False, [r_K[kb], qn_r[qb]], [pr])
                            pair.append((kt, pt, pr))
                        for dk, (kt, pt, pr) in enumerate(pair):
                            r0 = dk * 64
                            mm(pt[:, 0:n], krT[r0:r0 + 64, kt * 128:(kt + 1) * 128], QrT[r0:r0 + 64, qb, 0:n], False, True,
                               [r_kr[grp_of_tile(kt)], qr_r[qb]], [pr])
                        for dk, (kt, pt, pr) in enumerate(pair):
                            pb = pcount[0] % 8
                            pcount[0] += 1
                            S.op(act, lambda: nc.scalar.activation(out=PT[:, pb, 0:n], in_=pt[:, 0:n], func=AF.Exp),
                                 [pr], [pt_r[pb]])
                            pend.append((ki0 + dk, kt, pb))
                        while len(pend) > PIPE_DEPTH:
                            pv(*pend.pop(0))
                    while pend:
                        pv(*pend.pop(0))
                    for j in range(nq):
                        at, ar, c0 = acc[j]
                        oj = (idx % 2) * 4 + j
                        S.op(dve, lambda: nc.vector.reciprocal(out=rec[:, oj:oj + 1], in_=at[:, c0 + 128:c0 + 129]), [ar], [r_rec])
                        S.op(dve, lambda: nc.vector.tensor_scalar(out=osb[:, oj, :], in0=at[:, c0:c0 + 128], scalar1=rec[:, oj:oj + 1],
                                                                  scalar2=None, op0=ALU.mult), [ar, r_rec], [osb_r[oj]])
                        S.dma(pool, oD[t0 + j * 128:t0 + (j + 1) * 128, h * 128:(h + 1) * 128], osb[:, oj, :],
                              [osb_r[oj]], [DR("o", (t0 + j * 128) // 128, h)])
                ps_set[0] = list(range(8))
                S.barrier()
            with nc.sbuf_tensor("L%d_" % l + "ot", [128, 2, 1024], F32) as ot, \
                    nc.sbuf_tensor("L%d_" % l + "gmt", [128, 2, 1024], BF16) as gmt, \
                    nc.sbuf_tensor("L%d_" % l + "ysq", [128, 1024], BF16) as ysq, \
                    nc.sbuf_tensor("L%d_" % l + "ybf", [128, 2, 1024], BF16) as ybf, \
                    nc.sbuf_tensor("L%d_" % l + "ymst", [128, 2, 8, 128], BF16) as ymst, \
                    nc.sbuf_tensor("L%d_" % l + "est", [128, 4], F32) as est:
                ot_r = [Res(), Res()]
                gm_r = [Res(), Res()]
                yb_r = [Res(), Res()]
                ym_r = [Res(), Res()]
                r_ysq, r_est = Res(), Res()
                tiles = list(range(34)) if do_ctx_q else list(range(2, 34))
                for ti, tl in enumerate(tiles):
                    i = ti % 2
                    S.dma(sp, ot[:, i, :], oD[tl * 128:(tl + 1) * 128, :], [DR("o", tl, h) for h in range(H)], [ot_r[i]])
                    S.dma(sp, gmt[:, i, :], gmD[tl * 128:(tl + 1) * 128, :], [DR("gm", tl)], [gm_r[i]])
                    S.op(pool, lambda: nc.gpsimd.memset(est[:, 0:1], 0.0), [], [r_est])
                    S.op(act, lambda: nc.scalar.activation(out=ysq[:], in_=ot[:, i, :], func=AF.Square, accum_out=est[:, 0:1]),
                         [ot_r[i]], [r_ysq, r_est])
                    S.op(act, lambda: nc.scalar.activation(out=est[:, 1:2], in_=est[:, 0:1], func=AF.Sqrt, scale=1.0 / 1024.0, bias=EPS),
                         [r_est], [r_est])
                    S.op(dve, lambda: nc.vector.reciprocal(out=rmAll[:, tl:tl + 1], in_=est[:, 1:2]), [r_est], [r_rm])
                    S.op(dve, lambda: nc.vector.tensor_tensor(out=ybf[:, i, :], in0=ot[:, i, :], in1=gmt[:, i, :], op=ALU.mult),
                         [ot_r[i], gm_r[i]], [yb_r[i]])
                    for kq in range(2):
                        pt, pr = ps_next()
                        ptb = pt[:].bitcast(BF16)
                        for kk in range(4):
                            k = kq * 4 + kk
                            S.op(pe, lambda: nc.tensor.transpose(ptb[:, kk * 128:(kk + 1) * 128], ybf[:, i, k * 128:(k + 1) * 128], ident[:]),
                                 [yb_r[i], r_const], [pr], inc=(kk == 3))
                        copy_on(ev_eng(), ymst[:, i, kq * 4:(kq + 1) * 4, :],
                                ptb[:, 0:512].rearrange("p (a b) -> p a b", b=128), [pr], [ym_r[i]])
                    S.dma(pool, ymD[:, tl * 128:(tl + 1) * 128].rearrange("(k p) t -> p k t", p=128), ymst[:, i],
                          [ym_r[i]], [DR("ym", tl)])
                S.barrier()
            if "B" in taps and l == 0:
                S.dma(sp, tap("o", [T, 1024])[:, :], oD[:, :], [], [DR("tapo")])
                S.dma(sp, tap("ym", [1024, T], BF16)[:, :], ymD[:, :], [], [DR("tapym")])
                S.dma(sp, tap("rm", [128, 34])[:, :], rmAll[:], [r_rm], [DR("taprm")])
                S.barrier()
            if stop == "B":
                break

        S.op(pool, lambda: nc.gpsimd.memset(sshAll[:], 0.0), [], [r_ssh])

        def hyena_seq(seq_t0, L, tg, B):
            nt = L // 128
            Ls = L // B
            nts = Ls // 128
            nF = nts + 1
            ne_blk = nts // 2 + 1
            pcw = min(512, Ls)
            npc = Ls // pcw
            nD = 2 * B - 1
            tabf, tabt = tabF[Ls], tabT[Ls]
            with nc.sbuf_tensor(tg + "w1s", [33, 64], F32) as w1s, \
                    nc.sbuf_tensor(tg + "w2s", [64, 64], F32) as w2s, \
                    nc.sbuf_tensor(tg + "w3s", [64, 2048], F32) as w3s, \
                    nc.sbuf_tensor(tg + "fv", [64, 3], F32) as fv, \
                    nc.sbuf_tensor(tg + "zt", [33, 2, 512], F32) as zt, \
                    nc.sbuf_tensor(tg + "fa", [64, 2, 512], F32) as fa, \
                    nc.sbuf_tensor(tg + "h1", [64, 512], F32) as h1, \
                    nc.sbuf_tensor(tg + "h2", [64, 2, L], F32) as h2, \
                    nc.sbuf_tensor(tg + "absd", [128, 1024], F32) as absd, \
                    nc.sbuf_tensor(tg + "negt", [128, 2, nt], F32) as negt, \
                    nc.sbuf_tensor(tg + "wfc", [128, nF], F32) as wfc, \
                    nc.sbuf_tensor(tg + "dec", [128, 2, 512], F32) as dec, \
                    nc.sbuf_tensor(tg + "SEG", [128, 2 * nt, 512], BF16) as SEG, \
                    nc.sbuf_tensor(tg + "kAB", [128, 1, 2, nts, 512], BF16) as kAB, \
                    nc.sbuf_tensor(tg + "slab", [128, 2, nts, 2, 128], BF16) as slab, \
                    nc.sbuf_tensor(tg + "kst", [128, 2, 2, 512], BF16) as kst:
                r_fw, r_fa, r_h1, r_h2, r_cst, r_seg = Res(), Res(), Res(), Res(), Res(), Res()
                zt_r = [Res(), Res()]
                dec_r = [Res(), Res()]
                kab_r = [Res(), Res()]
                slab_r = [Res(), Res()]
                kst_r = [Res(), Res()]
                S.dma(sp, w1s[:], filt_w1[:, l, :], [], [r_fw])
                S.dma(sp, w2s[:], filt_w2[:, l, :], [], [r_fw])
                S.dma(sp, w3s[:], filt_w3[:, l, :], [], [r_fw])
                S.dma(sp, fv[:], filt_v[:, l, :], [], [r_fw])
                S.dma(sp, absd[:], absd_in[0:1, :].partition_broadcast(128), [], [r_cst])
                S.dma(sp, negt[:], negt_in[L][:, :, :], [], [r_cst])
                S.dma(sp, wfc[:], wf_in[Ls][:, :], [], [r_cst])
                zc = [0]

                def sin_layer(ps_ap, pr, bcol, out_ap, out_res, n):
                    S.op(dve, lambda: nc.vector.tensor_scalar(out=fa[:, 0, 0:n], in0=ps_ap, scalar1=fv[:, bcol:bcol + 1],
                                                              scalar2=fv[:, 1:2], op0=ALU.add, op1=ALU.mult), [pr, r_fw], [r_fa])
                    S.op(dve, lambda: nc.vector.tensor_scalar(out=fa[:, 1, 0:n], in0=fa[:, 0, 0:n], scalar1=1.0 / TWO_PI,
                                                              scalar2=MAGIC, op0=ALU.mult, op1=ALU.add), [r_fa], [r_fa])
                    S.op(dve, lambda: nc.vector.tensor_scalar(out=fa[:, 1, 0:n], in0=fa[:, 1, 0:n], scalar1=MAGIC,
                                                              scalar2=-TWO_PI, op0=ALU.subtract, op1=ALU.mult), [r_fa], [r_fa])
                    S.op(dve, lambda: nc.vector.tensor_tensor(out=fa[:, 0, 0:n], in0=fa[:, 0, 0:n], in1=fa[:, 1, 0:n], op=ALU.add),
                         [r_fa], [r_fa])
                    S.op(act, lambda: nc.scalar.activation(out=out_ap, in_=fa[:, 0, 0:n], func=AF.Sin), [r_fa], [out_res])

                fpw = min(512, L)
                for dr in range(2):
                    for pc in range(L // fpw):
                        zi = zc[0] % 2
                        zc[0] += 1
                        S.dma(sp, zt[:, zi, 0:fpw], zT_in[L][dr, :, pc * fpw:(pc + 1) * fpw], [], [zt_r[zi]])
                        pt, pr = ps_next()
                        mm(pt[0:64, 0:fpw], w1s[:], zt[:, zi, 0:fpw], True, True, [r_fw, zt_r[zi]], [pr])
                        sin_layer(pt[0:64, 0:fpw], pr, 0, h1[:, 0:fpw], r_h1, fpw)
                        pt, pr = ps_next()
                        mm(pt[0:64, 0:fpw], w2s[:], h1[:, 0:fpw], True, True, [r_fw, r_h1], [pr])
                        sin_layer(pt[0:64, 0:fpw], pr, 2, h2[:, dr, pc * fpw:(pc + 1) * fpw], r_h2, fpw)
                sc = [0]
                kc = [0]
                for hf in range(2):
                    for tt in range(nt):
                        for dr in range(2):
                            pt, pr = ps_next()
                            mm(pt[:, :], h2[:, dr, tt * 128:(tt + 1) * 128], w3s[:, dr * 1024 + hf * 512:dr * 1024 + (hf + 1) * 512],
                               True, True, [r_h2, r_fw], [pr])
                            S.op(act, lambda: nc.scalar.activation(out=dec[:, dr, :], in_=absd[:, hf * 512:(hf + 1) * 512], func=AF.Exp,
                                                                   scale=negt[:, dr, tt:tt + 1]), [r_cst], [dec_r[dr]])
                            S.op(dve, lambda: nc.vector.tensor_tensor(out=SEG[:, (1 - dr) * nt + tt, :], in0=pt[:, :], in1=dec[:, dr, :],
                                                                      op=ALU.mult), [pr, dec_r[dr]], [r_seg])
                    for dd in range(nD):
                        d = dd - (B - 1)
                        e1 = (d + B) * nts
                        e0 = (d - 1 + B) * nts
                        ki = 0
                        kc[0] += 1
                        S.op(pool, lambda: nc.gpsimd.tensor_tensor(out=kAB[:, ki, 0], in0=SEG[:, e1:e1 + nts, :], in1=SEG[:, e0:e0 + nts, :],
                                                                   op=ALU.add), [r_seg], [kab_r[ki]])
                        S.op(dve, lambda: nc.vector.tensor_tensor(out=kAB[:, ki, 1], in0=SEG[:, e1:e1 + nts, :], in1=SEG[:, e0:e0 + nts, :],
                                                                  op=ALU.subtract), [r_seg], [kab_r[ki]])
                        for j in range(nF):
                            si = sc[0] % 2
                            sc[0] += 1
                            par = 0 if j < ne_blk else 1
                            S.dma(sp, slab[:, si], tabf[0:Ls, j].rearrange("(i p) c f -> p i c f", p=128), [], [slab_r[si]])
                            accs = [ps_next() for _ in range(2)]
                            for cs in range(2):
                                for i in range(nts):
                                    mm(accs[cs][0][:, :], slab[:, si, i, cs, :], kAB[:, ki, par, i, :], i == 0, i == nts - 1,
                                       [slab_r[si], kab_r[ki]], [accs[cs][1]])
                            for cs in range(2):
                                S.op(act, lambda: nc.scalar.activation(out=kst[:, si, cs, :], in_=accs[cs][0][:, :], func=AF.Copy,
                                                                       scale=wfc[:, j:j + 1]), [accs[cs][1], r_cst], [kst_r[si]])
                            r0 = (dd * nF + j) * 128
                            S.dma(pool, KreD[r0:r0 + 128, hf * 512:(hf + 1) * 512], kst[:, si, 0, :], [kst_r[si]], [DR("Kf", dd, j, hf, 0)])
                            S.dma(pool, KimD[r0:r0 + 128, hf * 512:(hf + 1) * 512], kst[:, si, 1, :], [kst_r[si]], [DR("Kf", dd, j, hf, 1)])
                S.barrier()
            if DBG.get("cstage") == 0:
                return
            with nc.sbuf_tensor(tg + "uTM", [128, nt, 512], BF16) as uTM, \
                    nc.sbuf_tensor(tg + "Y", [128, B, 2, nF, 512], BF16) as Y:
                for hf in range(2):
                    r_u = Res()
                    r_Y = Res()
                    with nc.sbuf_tensor(tg + "hin%d" % hf, [128, 3, L + 2], BF16) as hin, \
                            nc.sbuf_tensor(tg + "ct%d" % hf, [128, 3, L], F32) as ct, \
                            nc.sbuf_tensor(tg + "ubf%d" % hf, [128, L], BF16) as ubf:
                        r_hin, r_ct, r_ubf = Res(), Res(), Res()
                        S.op(pool, lambda: nc.gpsimd.memset(hin[:, :, 0:1], 0.0), [], [r_hin])
                        S.op(pool, lambda: nc.gpsimd.memset(hin[:, :, L + 1:L + 2], 0.0), [], [r_hin])
                        for cbk in range(4):
                            cb = hf * 4 + cbk
                            for jp in range(3):
                                row0 = (jp * 8 + cb) * 128
                                S.dma(sp, hin[:, jp, 1:L + 1], hyD[row0:row0 + 128, seq_t0:seq_t0 + L], [], [r_hin])
                            for jp in range(3):
                                q = jp * 8 + cb
                                w0 = vecs[:, l, V_CW + q:V_CW + q + 1]
                                w1 = vecs[:, l, V_CW + 24 + q:V_CW + 24 + q + 1]
                                w2 = vecs[:, l, V_CW + 48 + q:V_CW + 48 + q + 1]
                                bb = vecs[:, l, V_CB + q:V_CB + q + 1]
                                S.op(act, lambda: nc.scalar.activation(out=ct[:, jp, :], in_=hin[:, jp, 1:L + 1], func=AF.Identity,
                                                                       scale=w1, bias=bb), [r_hin, r_const], [r_ct])
                                S.op(dve, lambda: nc.vector.scalar_tensor_tensor(out=ct[:, jp, :], in0=hin[:, jp, 0:L], scalar=w0,
                                                                                 in1=ct[:, jp, :], op0=ALU.mult, op1=ALU.add),
                                     [r_hin, r_const, r_ct], [r_ct])
                                S.op(dve, lambda: nc.vector.scalar_tensor_tensor(out=ct[:, jp, :], in0=hin[:, jp, 2:L + 2], scalar=w2,
                                                                                 in1=ct[:, jp, :], op0=ALU.mult, op1=ALU.add),
                                     [r_hin, r_const, r_ct], [r_ct])
                            S.op(pool, lambda: nc.gpsimd.tensor_tensor(out=ct[:, 1, :], in0=ct[:, 1, :], in1=ct[:, 2, :], op=ALU.mult),
                                 [r_ct], [r_ct])
                            S.op(act, lambda: nc.scalar.copy(out=ubf[:], in_=ct[:, 1, :]), [r_ct], [r_ubf])
                            S.dma(pool, x0D[cb * 128:(cb + 1) * 128, 0:L], ct[:, 0, :], [r_ct], [DR("x0", cb)])
                            S.dma(pool, uD[cb * 128:(cb + 1) * 128, 0:L], ct[:, 1, :], [r_ct], [DR("u", cb)])
                            for t4 in range(0, nt, 4):
                                nb = min(4, nt - t4)
                                pt, pr = ps_next()
                                ptb = pt[:].bitcast(BF16)
                                for jj in range(nb):
                                    S.op(pe, lambda: nc.tensor.transpose(ptb[:, jj * 128:(jj + 1) * 128],
                                                                         ubf[:, (t4 + jj) * 128:(t4 + jj + 1) * 128], ident[:]),
                                         [r_ubf, r_const], [pr], inc=(jj == nb - 1))
                                copy_on(ev_eng(), uTM[:, t4:t4 + nb, cbk * 128:(cbk + 1) * 128],
                                        ptb[:, 0:nb * 128].rearrange("p (a b) -> p a b", b=128), [pr], [r_u])
                        S.barrier()
                    if DBG.get("cstage") == 1:
                        continue
                    with nc.sbuf_tensor(tg + "slb%d" % hf, [128, 2, nts, 2, 128], BF16) as slab, \
                            nc.sbuf_tensor(tg + "kf%d" % hf, [128, 2, nD, 2, 512], BF16) as kf, \
                            nc.sbuf_tensor(tg + "us%d" % hf, [128, 2, B, 2, 512], F32) as ust, \
                            nc.sbuf_tensor(tg + "pr%d" % hf, [128, 2, 4, 512], F32) as prd, \
                            nc.sbuf_tensor(tg + "ya%d" % hf, [128, 2, 2, 512], F32) as yacc:
                        slab_r = [Res(), Res()]
                        kf_r = [Res(), Res()]
                        yacc_r = [Res(), Res()]
                        ust_r2 = [[Res() for _ in range(B)] for _ in range(2)]
                        prd_r = [Res(), Res()]
                        pq = [0]
                        for j in range(nF):
                            si = j % 2
                            S.dma(sp, slab[:, si], tabf[0:Ls, j].rearrange("(i p) c f -> p i c f", p=128), [], [slab_r[si]])
                            ust_r = ust_r2[si]
                            r_kf = kf_r[si]
                            for dd in range(nD):
                                r0 = (dd * nF + j) * 128
                                S.dma(act, kf[:, si, dd, 0, :], KreD[r0:r0 + 128, hf * 512:(hf + 1) * 512], [DR("Kf", dd, j, hf, 0)], [r_kf])
                                S.dma(act, kf[:, si, dd, 1, :], KimD[r0:r0 + 128, hf * 512:(hf + 1) * 512], [DR("Kf", dd, j, hf, 1)], [r_kf])
                            for i in range(B):
                                accs = [ps_next() for _ in range(2)]
                                for cs in range(2):
                                    for ic in range(nts):
                                        mm(accs[cs][0][:, :], slab[:, si, ic, cs, :], uTM[:, i * nts + ic, :], ic == 0, ic == nts - 1,
                                           [slab_r[si], r_u], [accs[cs][1]])
                                copy_on(act, ust[:, si, i, 0, :], accs[0][0][:, :], [accs[0][1]], [ust_r[i]])
                                copy_on(dve, ust[:, si, i, 1, :], accs[1][0][:, :], [accs[1][1]], [ust_r[i]])
                            for o in range(B):
                                E = dve if (o % 2 == 0) else pool
                                eb = o % 2
                                for i in range(B):
                                    dd = (o - i) + (B - 1)
                                    last_i = (i == B - 1)
                                    for q, (a_i, k_i) in enumerate(((0, 0), (1, 1), (0, 1), (1, 0))):
                                        S.op(E, lambda: E.e.tensor_tensor(out=prd[:, eb, q, :], in0=ust[:, si, i, a_i, :], in1=kf[:, si, dd, k_i, :],
                                                                          op=ALU.mult), [ust_r[i], r_kf], [prd_r[eb]])
                                    dre = Y[:, o, 0, j, :] if last_i else yacc[:, eb, 0, :]
                                    dim_ = Y[:, o, 1, j, :] if last_i else yacc[:, eb, 1, :]
                                    wr = [r_Y] if last_i else [yacc_r[eb]]
                                    if i == 0:
                                        S.op(E, lambda: E.e.tensor_tensor(out=dre, in0=prd[:, eb, 0, :], in1=prd[:, eb, 1, :], op=ALU.subtract),
                                             [prd_r[eb]], wr)
                                        S.op(E, lambda: E.e.tensor_tensor(out=dim_, in0=prd[:, eb, 2, :], in1=prd[:, eb, 3, :], op=ALU.add),
                                             [prd_r[eb]], wr)
                                    else:
                                        S.op(E, lambda: E.e.tensor_tensor(out=prd[:, eb, 0, :], in0=prd[:, eb, 0, :], in1=prd[:, eb, 1, :],
                                                                          op=ALU.subtract), [prd_r[eb]], [prd_r[eb]])
                                        S.op(E, lambda: E.e.tensor_tensor(out=prd[:, eb, 2, :], in0=prd[:, eb, 2, :], in1=prd[:, eb, 3, :],
                                                                          op=ALU.add), [prd_r[eb]], [prd_r[eb]])
                                        S.op(E, lambda: E.e.tensor_tensor(out=dre, in0=yacc[:, eb, 0, :], in1=prd[:, eb, 0, :], op=ALU.add),
                                             [prd_r[eb], yacc_r[eb]], wr)
                                        S.op(E, lambda: E.e.tensor_tensor(out=dim_, in0=yacc[:, eb, 1, :], in1=prd[:, eb, 2, :], op=ALU.add),
                                             [prd_r[eb], yacc_r[eb]], wr)
                        S.barrier()
                    with nc.sbuf_tensor(tg + "pc%d" % hf, [128, 5, 2, 512], BF16) as pcs, \
                            nc.sbuf_tensor(tg + "ex%d" % hf, [128, 4, 2, 512], F32) as ex, \
                            nc.sbuf_tensor(tg + "eg%d" % hf, [128, 4, 512], BF16) as eg, \
                            nc.sbuf_tensor(tg + "ey%d" % hf, [128, 4, 512], F32) as ey, \
                            nc.sbuf_tensor(tg + "es%d" % hf, [128, 2, 512], BF16) as es, \
                            nc.sbuf_tensor(tg + "eo%d" % hf, [128, 2, 512], BF16) as eo:
                        pcs_r = [Res() for _ in range(5)]
                        ex_r = [Res() for _ in range(4)]
                        eg_r = [Res() for _ in range(4)]
                        ey_r = [Res() for _ in range(4)]
                        es_r = [Res() for _ in range(2)]
                        eo_r = [Res() for _ in range(2)]
                        pcc = [0]
                        for o in range(B):
                            for tb in range(npc):
                                c0 = o * Ls + tb * pcw
                                for cbk in range(4):
                                    cb = hf * 4 + cbk
                                    S.dma(pool, ex[:, cbk, 0, 0:pcw], x0D[cb * 128:(cb + 1) * 128, c0:c0 + pcw], [DR("x0", cb)], [ex_r[cbk]])
                                    S.dma(pool, ex[:, cbk, 1, 0:pcw], uD[cb * 128:(cb + 1) * 128, c0:c0 + pcw], [DR("u", cb)], [ex_r[cbk]])
                                    S.dma(pool, eg[:, cbk, 0:pcw], ghD[cb * 128:(cb + 1) * 128, seq_t0 + c0:seq_t0 + c0 + pcw], [], [eg_r[cbk]])
                                accs = [ps_next() for _ in range(4)]
                                for j in range(nF):
                                    pi = pcc[0] % 5
                                    pcc[0] += 1
                                    S.dma(sp if j % 2 == 0 else act, pcs[:, pi, :, 0:pcw],
                                          tabt[j * 128:(j + 1) * 128, :, tb * pcw:(tb + 1) * pcw], [], [pcs_r[pi]])
                                    for cbk in range(4):
                                        mm(accs[cbk][0][:, 0:pcw], Y[:, o, 0, j, cbk * 128:(cbk + 1) * 128], pcs[:, pi, 0, 0:pcw], j == 0, False,
                                           [r_Y, pcs_r[pi]], [accs[cbk][1]])
                                        mm(accs[cbk][0][:, 0:pcw], Y[:, o, 1, j, cbk * 128:(cbk + 1) * 128], pcs[:, pi, 1, 0:pcw], False, j == nF - 1,
                                           [r_Y, pcs_r[pi]], [accs[cbk][1]], inc=(cbk == 3 or j == nF - 1))
                                for cbk in range(4):
                                    cb = hf * 4 + cbk
                                    ei = cbk
                                    S.op(dve, lambda: nc.vector.scalar_tensor_tensor(out=ey[:, ei, 0:pcw], in0=ex[:, ei, 1, 0:pcw],
                                                                                     scalar=vecs[:, l, V_HD + cb:V_HD + cb + 1],
                                                                                     in1=accs[cbk][0][:, 0:pcw], op0=ALU.mult, op1=ALU.add),
                                         [ex_r[ei], accs[cbk][1], r_const], [ey_r[ei]])
                                    S.op(pool, lambda: nc.gpsimd.tensor_tensor(out=ey[:, ei, 0:pcw], in0=ey[:, ei, 0:pcw], in1=ex[:, ei, 0, 0:pcw],
                                                                               op=ALU.mult), [ex_r[ei], ey_r[ei]], [ey_r[ei]])
                                    S.op(act, lambda: nc.scalar.activation(out=es[:, ei % 2, 0:pcw], in_=ey[:, ei, 0:pcw], func=AF.Square),
                                         [ey_r[ei]], [es_r[ei % 2]])
                                    S.op(pool, lambda: nc.gpsimd.tensor_tensor(out=eo[:, ei % 2, 0:pcw], in0=ey[:, ei, 0:pcw], in1=eg[:, ei, 0:pcw],
                                                                               op=ALU.mult), [ey_r[ei], eg_r[ei]], [eo_r[ei % 2]])
                                    S.dma(pool, yhD[cb * 128:(cb + 1) * 128, seq_t0 + c0:seq_t0 + c0 + pcw], eo[:, ei % 2, 0:pcw], [eo_r[ei % 2]],
                                          [DR("yh", cb, (seq_t0 + c0) // 128)])
                                    nq = pcw // 128
                                    pt, pr = ps_next()
                                    for q in range(nq):
                                        mm(pt[:, 8 * q:8 * q + 8], es[:, ei % 2, q * 128:(q + 1) * 128], ones_bf[:, 0:8], True, True,
                                           [es_r[ei % 2], r_const], [pr])
                                    tile0 = (seq_t0 + c0) // 128
                                    S.op(dve, lambda: nc.vector.tensor_tensor(out=sshAll[:, tile0:tile0 + nq], in0=pt[:, 0:8 * nq:8],
                                                                              in1=sshAll[:, tile0:tile0 + nq], op=ALU.add), [pr, r_ssh], [r_ssh])
                        S.barrier()

        if not last and not DBG.get("skipctx"):
            hyena_seq(0, LC, "c%d" % l, 1)
        hyena_seq(LC, LX, "x%d" % l, HB)
        with nc.sbuf_tensor("L%d_" % l + "rht", [128, 34], F32) as rht:
            r_rht = Res()
            S.op(act, lambda: nc.scalar.activation(out=rht[:], in_=sshAll[:], func=AF.Sqrt, scale=1.0 / 1024.0, bias=EPS), [r_ssh], [r_rht])
            S.op(dve, lambda: nc.vector.reciprocal(out=rhAll[:], in_=rht[:]), [r_rht], [r_ssh])
            S.barrier()
        if "C" in taps and l == 0:
            S.dma(sp, tap("yh", [1024, T], BF16)[:, :], yhD[:, :], [], [DR("tapyh")])
            S.dma(sp, tap("rh", [128, 34])[:, :], rhAll[:], [r_ssh], [DR("taprh")])
            if "Ck" in taps:
                S.dma(sp, tap("Kre", [4224, 1024])[:, :], KreD[:, :], [], [DR("tapkre")])
                S.dma(sp, tap("Kim", [4224, 1024])[:, :], KimD[:, :], [], [DR("tapkim")])
            if "Cu" in taps:
                S.dma(sp, tap("u", [1024, LX])[:, :], uD[:, :], [], [DR("tapu")])
                S.dma(sp, tap("x0", [1024, LX])[:, :], x0D[:, :], [], [DR("tapx0")])
            S.barrier()
        if stop == "C":
            break

        with nc.sbuf_tensor("L%d_" % l + "wout_s", [128, 16, D], BF16) as wout_s, \
                nc.sbuf_tensor("L%d_" % l + "ymt", [128, 2, 8, 128], BF16) as ymt, \
                nc.sbuf_tensor("L%d_" % l + "yht", [128, 2, 8, 128], BF16) as yht, \
                nc.sbuf_tensor("L%d_" % l + "xr", [128, 2, D], F32) as xr, \
                nc.sbuf_tensor("L%d_" % l + "zt", [128, 2, D], F32) as zt, \
                nc.sbuf_tensor("L%d_" % l + "zsq", [128, D], BF16) as zsq, \
                nc.sbuf_tensor("L%d_" % l + "dst", [128, 4], F32) as dst:
            r_wo, r_zsq, r_dst = Res(), Res(), Res()
            ym_r = [Res(), Res()]
            yh_r = [Res(), Res()]
            xr_r = [Res(), Res()]
            z_r = [Res(), Res()]
            S.dma(sp, wout_s[:], wout_b[:, :].rearrange("(k p) c -> p k c", p=128), [DR("wout", k, 0) for k in range(16)], [r_wo])
            tiles = list(range(34)) if not last else list(range(2, 34))
            for ti, tl in enumerate(tiles):
                i = ti % 2
                isx = tl >= 2
                S.dma(sp, ymt[:, i], ymD[:, tl * 128:(tl + 1) * 128].rearrange("(k p) t -> p k t", p=128), [DR("ym", tl)], [ym_r[i]])
                S.dma(sp, yht[:, i], yhD[:, tl * 128:(tl + 1) * 128].rearrange("(k p) t -> p k t", p=128),
                      [DR("yh", cb, tl) for cb in range(8)], [yh_r[i]])
                rsrc = xsrc[(tl - 2) * 128:(tl - 1) * 128, :] if isx else csrc[tl * 128:(tl + 1) * 128, :]
                S.dma(sp, xr[:, i, :], rsrc, [DR("xres", tl)], [xr_r[i]])
                for cbk in range(4):
                    c0 = cbk * 512
                    pt, pr = ps_next()
                    for k in range(8):
                        mm(pt[:, :], ymt[:, i, k, :], wout_s[:, k, c0:c0 + 512], k == 0, k == 7, [ym_r[i], r_wo], [pr])
                    pt2, pr2 = ps_next()
                    for k in range(8):
                        mm(pt2[:, :], yht[:, i, k, :], wout_s[:, 8 + k, c0:c0 + 512], k == 0, k == 7, [yh_r[i], r_wo], [pr2])
                    S.op(act, lambda: nc.scalar.activation(out=zt[:, i, c0:c0 + 512], in_=pt[:, :], func=AF.Copy,
                                                           scale=rmAll[:, tl:tl + 1]), [pr, r_rm], [z_r[i]])
                    S.op(dve, lambda: nc.vector.scalar_tensor_tensor(out=zt[:, i, c0:c0 + 512], in0=pt2[:, :], scalar=rhAll[:, tl:tl + 1],
                                                                     in1=zt[:, i, c0:c0 + 512], op0=ALU.mult, op1=ALU.add),
                         [pr2, r_ssh, z_r[i]], [z_r[i]])
                S.op(pool, lambda: nc.gpsimd.memset(dst[:, 0:1], 0.0), [], [r_dst])
                S.op(act, lambda: nc.scalar.activation(out=zsq[:], in_=zt[:, i, :], func=AF.Square, accum_out=dst[:, 0:1]),
                     [z_r[i]], [r_zsq, r_dst])
                S.op(act, lambda: nc.scalar.activation(out=dst[:, 1:2], in_=dst[:, 0:1], func=AF.Sqrt, scale=1.0 / D, bias=EPS),
                     [r_dst], [r_dst])
                S.op(dve, lambda: nc.vector.reciprocal(out=dst[:, 2:3], in_=dst[:, 1:2]), [r_dst], [r_dst])
                gp = gpx if isx else gpc
                S.op(dve, lambda: nc.vector.scalar_tensor_tensor(out=zt[:, i, :], in0=zt[:, i, :], scalar=dst[:, 2:3], in1=gp[:],
                                                                 op0=ALU.mult, op1=ALU.mult), [z_r[i], r_dst, r_gp], [z_r[i]])
                S.op(pool, lambda: nc.gpsimd.tensor_tensor(out=zt[:, i, :], in0=zt[:, i, :], in1=xr[:, i, :], op=ALU.add),
                     [z_r[i], xr_r[i]], [z_r[i]])
                ddst = xdst[(tl - 2) * 128:(tl - 1) * 128, :] if isx else ctx1D[tl * 128:(tl + 1) * 128, :]
                S.dma(pool, ddst, zt[:, i, :], [z_r[i]], [DR("xres", tl)])
            S.barrier()
        if "D" in taps and l == 0:
            S.dma(sp, tap("x1", [LX, D])[:, :], x1D[:, :], [], [DR("tapx1")])
            S.dma(sp, tap("c1", [LC, D])[:, :], ctx1D[:, :], [], [DR("tapc1")])
            S.barrier()
        if stop == "D":
            break
    S.barrier()
    return nc, tap_out


def _pack_vecs(inp):
    v = np.zeros((128, DEPTH, 160), np.float32)
    for l in range(DEPTH):
        v[:, l, 0:16] = _cols(inp["pre_g"][l], 16)
        v[:, l, 16:20] = _cols(inp["q_norm_g"][l], 4)
        v[:, l, 20:22] = _cols(inp["kv_norm_g"][l], 2)
        v[:, l, 22:38] = _cols(np.concatenate([inp["grp_g_mla"][l], inp["grp_g_hy"][l]]), 16)
        for j in range(3):
            v[:, l, 38 + 24 * j:38 + 24 * (j + 1)] = _cols(inp["conv_w"][l, j], 24)
        v[:, l, 110:134] = _cols(inp["conv_b"][l], 24)
        v[:, l, 134:142] = _cols(inp["hy_D"][l], 8)
    return v


_CONST_CACHE = {}


def _host_consts():
    if not _CONST_CACHE:
        c = {}
        cosT, sinT, Rm = _rope_consts()
        Rm2 = np.zeros((128, 128), np.float32)
        Rm2[0:64, 0:64] = Rm
        Rm2[64:128, 64:128] = Rm
        c["ropeC"], c["ropeS"], c["Rm"] = np.concatenate([cosT, cosT], 0), np.concatenate([sinT, sinT], 0), Rm2
        deltas = np.linspace(math.log(1e-2) / 0.3, math.log(1e-2) / 1.5, 1024, dtype=np.float32)
        c["absd"] = np.abs(deltas).reshape(1, 1024).astype(np.float32)
        c["ident"] = np.eye(128, dtype=np.float32).astype(ml_dtypes.bfloat16)
        c["sgn"] = np.where(np.arange(128) % 2 == 0, 1.0, -1.0).astype(np.float32).reshape(128, 1)
        for L, Ls, s in ((LX, LX // HB, "x"), (LC, LC, "c")):
            c["tabF_" + s], c["tabT_" + s] = _dft_tables(Ls)
            c["zT_" + s], c["negt_" + s], _ = _filter_consts(L)
            c["wf_" + s] = _filter_consts(Ls)[2]
        _CONST_CACHE.update(c)
    return _CONST_CACHE


def make_in_maps(inp, ncores=NCORES):
    c = _host_consts()
    shared = dict(c)
    f32 = lambda a: np.ascontiguousarray(np.asarray(a, np.float32))
    for k in ("ada_w", "ada_b", "w_in", "w_uq", "w_ukv", "w_out", "post_g"):
        shared[k] = f32(inp[k])
    shared["vecs"] = _pack_vecs(inp)
    shared["filt_w1"] = f32(np.transpose(inp["filt_w1"], (1, 0, 2)))
    shared["filt_w2"] = f32(np.transpose(inp["filt_w2"], (1, 0, 2)))
    shared["filt_w3"] = f32(np.transpose(inp["filt_w3"], (1, 0, 2)))
    shared["filt_v"] = f32(np.stack([inp["filt_b1"].T, inp["filt_freq"].T, inp["filt_b2"].T], axis=-1))
    maps = []
    for b in range(ncores):
        m = dict(shared)
        m["x"] = f32(inp["x"][b])
        m["ctx"] = f32(inp["ctx"][b])
        cT = np.stack([_cols(inp["c"][b], 16), _cols(inp["c_ctx"], 16)], axis=-1)
        m["cT"] = f32(cT)
        maps.append(m)
    return maps


def kernel(**inputs):
    nc, _ = build_program()
    maps = make_in_maps(inputs)
    res = run_bass_kernel_spmd(nc, maps, core_ids=list(range(NCORES)))
    return np.stack([np.asarray(res.results[b]["out"], np.float32) for b in range(NCORES)], axis=0)
```
